# Optimizing a Trainium2 kernel written in Bass

```python
import jax, jax.numpy as jnp
from jax import lax
import numpy as np

D_MODEL = 1024
BATCH = 8
SEQ = 2048
DEPTH = 1
DEC_BATCH = 32
DEC_SEQ = 4
PAST_LEN = 8192
PAGE_SIZE = 128

D_PLE = 256
A_GROUPS = 4
A_WIDTH = 128
A_CHUNK = 128
A_DIM = A_GROUPS * A_WIDTH
B_HEADS = 8
B_HEAD_DIM = 64
B_DIM = B_HEADS * B_HEAD_DIM
MOBA_BLOCK = 256
MOBA_TOPK = 3
Q_BLOCK = 64
D_FF = 4 * D_MODEL
IN_DIM = 2 * A_DIM + 3 * B_DIM
EPS = 1e-6

kernel_name = "hymba_gmlp_moba_decode_step"


def rms_norm(x, g):
    xf = x.astype(jnp.float32)
    y = xf * lax.rsqrt(jnp.mean(xf * xf, axis=-1, keepdims=True) + EPS)
    return (y * g.astype(jnp.float32)).astype(x.dtype)


def in_projection(hn, w_in, a_v_norm, q_norm, k_norm):
    z = hn @ w_in
    u, va, q, k, vb = jnp.split(z, [A_DIM, 2 * A_DIM, 2 * A_DIM + B_DIM, 2 * A_DIM + 2 * B_DIM], axis=-1)
    lead = hn.shape[:-1]
    u = jax.nn.gelu(u).reshape(*lead, A_GROUPS, A_WIDTH)
    va = rms_norm(jax.nn.gelu(va).reshape(*lead, A_GROUPS, A_WIDTH), a_v_norm)
    q = rms_norm(q.reshape(*lead, B_HEADS, B_HEAD_DIM), q_norm)
    k = rms_norm(k.reshape(*lead, B_HEADS, B_HEAD_DIM), k_norm)
    vb = vb.reshape(*lead, B_HEADS, B_HEAD_DIM)
    return u, va, q, k, vb


def spatial_gate_prompt(u, va, ws, bs):
    b, s = u.shape[:2]
    nc = s // A_CHUNK
    w = jnp.tril(ws)
    vc = va.reshape(b, nc, A_CHUNK, A_GROUPS, A_WIDTH)
    sg = jnp.einsum('gts,bcsgw->bctgw', w, vc) + bs.T[:, :, None]
    return (u * sg.reshape(b, s, A_GROUPS, A_WIDTH)).reshape(b, s, A_DIM)


def spatial_gate_sample(u, va, ws, bs):
    b, n = u.shape[:2]
    w = jnp.tril(ws)[:, :n, :n]
    sg = jnp.einsum('gts,bsgw->btgw', w, va) + bs[:, :n].T[:, :, None]
    return (u * sg).reshape(b, n, A_DIM)


def to_blocks(x):
    b, l = x.shape[:2]
    nb = -(-l // MOBA_BLOCK)
    x = jnp.pad(x, ((0, 0), (0, nb * MOBA_BLOCK - l), (0, 0), (0, 0)))
    return x.reshape(b, nb, MOBA_BLOCK, B_HEADS, B_HEAD_DIM).transpose(0, 3, 1, 2, 4)


def moba_queries(q, t, kb, vb, kmean):
    b, h, nq, d = q.shape
    nb = kb.shape[2]
    k_sel = min(MOBA_TOPK, nb)
    n_past = t // MOBA_BLOCK
    gate = jnp.einsum('bhqd,bhnd->bhqn', q.astype(jnp.float32), kmean)
    past = jnp.arange(nb)[None, :] < n_past[:, None]
    gate = jnp.where(past[None, None], gate, -jnp.inf)
    _, top = lax.top_k(gate, k_sel)
    own = jnp.broadcast_to(n_past[None, None, :, None], (b, h, nq, 1)).astype(top.dtype)
    sel = jnp.concatenate([top, own], axis=-1)
    slot = jnp.arange(k_sel + 1)
    slot_ok = (slot[None, :] < n_past[:, None]) | (slot[None, :] == k_sel)
    bi = jnp.arange(b)[:, None, None, None]
    hi = jnp.arange(h)[None, :, None, None]
    kg = kb[bi, hi, sel]
    vg = vb[bi, hi, sel]
    pos = sel[..., None] * MOBA_BLOCK + jnp.arange(MOBA_BLOCK)
    mask = slot_ok[None, None, :, :, None] & (pos <= t[None, None, :, None, None])
    s = jnp.einsum('bhqd,bhqnkd->bhqnk', q, kg).astype(jnp.float32) * (d ** -0.5)
    s = jnp.where(mask, s, -jnp.inf)
    p = jax.nn.softmax(s.reshape(b, h, nq, -1), axis=-1).reshape(s.shape).astype(vg.dtype)
    return jnp.einsum('bhqnk,bhqnkd->bhqd', p, vg)


def moba_prompt(q, k, v):
    b, s = q.shape[:2]
    kb, vb = to_blocks(k), to_blocks(v)
    kmean = jnp.mean(kb.astype(jnp.float32), axis=3)
    qt = q.transpose(0, 2, 1, 3)
    nc = s // Q_BLOCK

    def chunk(c):
        start = c * Q_BLOCK
        qc = lax.dynamic_slice_in_dim(qt, start, Q_BLOCK, axis=2)
        t = start + jnp.arange(Q_BLOCK, dtype=jnp.int32)
        return moba_queries(qc, t, kb, vb, kmean)

    o = lax.map(chunk, jnp.arange(nc, dtype=jnp.int32))
    return o.transpose(1, 0, 3, 2, 4).reshape(b, s, B_DIM)


def moba_sample(q, k, v, cache_k, cache_v, page_table):
    db, n = q.shape[:2]
    past_k = cache_k[page_table].reshape(db, -1, B_HEADS, B_HEAD_DIM).astype(k.dtype)
    past_v = cache_v[page_table].reshape(db, -1, B_HEADS, B_HEAD_DIM).astype(v.dtype)
    past_len = past_k.shape[1]
    kb = to_blocks(jnp.concatenate([past_k, k], axis=1))
    vb = to_blocks(jnp.concatenate([past_v, v], axis=1))
    kmean = jnp.mean(kb.astype(jnp.float32), axis=3)
    t = past_len + jnp.arange(n, dtype=jnp.int32)
    o = moba_queries(q.transpose(0, 2, 1, 3), t, kb, vb, kmean)
    return o.transpose(0, 2, 1, 3).reshape(db, n, B_DIM)


def channel_and_ple(h, p, ln2, w_up, w_down, ln3, w_ple_gate, w_ple_proj, ple_norm):
    h = h + jnp.square(jax.nn.relu(rms_norm(h, ln2) @ w_up)) @ w_down
    gate = jax.nn.sigmoid(rms_norm(h, ln3) @ w_ple_gate)
    e = rms_norm(p @ w_ple_proj, ple_norm)
    return h + gate * e


def setup_inputs(seed: int = 0) -> dict:
    key = jax.random.key(seed)
    ks = jax.random.split(key, 24)
    n_pages = PAST_LEN // PAGE_SIZE
    n_used = DEC_BATCH * n_pages
    n_pool = n_used + (n_used + 3) // 4
    f32 = jnp.float32
    nrm = lambda k, shp, sc: jax.random.normal(k, shp, f32) * sc
    gain = lambda k, shp: 1.0 + 0.05 * jax.random.normal(k, shp, f32)
    page_table = jax.random.permutation(ks[6], n_pool)[:n_used].reshape(DEC_BATCH, n_pages).astype(jnp.int32)
    return {
        "x_prompt": nrm(ks[0], (BATCH, SEQ, D_MODEL), 1.0),
        "x_sample": nrm(ks[1], (DEC_BATCH, DEC_SEQ, D_MODEL), 1.0),
        "p_prompt": nrm(ks[2], (DEPTH, BATCH, SEQ, D_PLE), 1.0),
        "p_sample": nrm(ks[3], (DEPTH, DEC_BATCH, DEC_SEQ, D_PLE), 1.0),
        "cache_k": nrm(ks[4], (DEPTH, n_pool, PAGE_SIZE, B_HEADS, B_HEAD_DIM), 1.0),
        "cache_v": nrm(ks[5], (DEPTH, n_pool, PAGE_SIZE, B_HEADS, B_HEAD_DIM), 1.0),
        "page_table": page_table,
        "ln1": gain(ks[7], (DEPTH, D_MODEL)),
        "w_in": nrm(ks[8], (DEPTH, D_MODEL, IN_DIM), D_MODEL ** -0.5),
        "a_v_norm": gain(ks[9], (DEPTH, A_GROUPS, A_WIDTH)),
        "a_ws": nrm(ks[10], (DEPTH, A_GROUPS, A_CHUNK, A_CHUNK), A_CHUNK ** -0.5),
        "a_bs": gain(ks[11], (DEPTH, A_GROUPS, A_CHUNK)),
        "q_norm": gain(ks[12], (DEPTH, B_HEAD_DIM)),
        "k_norm": gain(ks[13], (DEPTH, B_HEAD_DIM)),
        "w_out": nrm(ks[14], (DEPTH, A_DIM + B_DIM, D_MODEL), (A_DIM + B_DIM) ** -0.5),
        "ln2": gain(ks[15], (DEPTH, D_MODEL)),
        "w_up": nrm(ks[16], (DEPTH, D_MODEL, D_FF), D_MODEL ** -0.5),
        "w_down": nrm(ks[17], (DEPTH, D_FF, D_MODEL), D_FF ** -0.5),
        "ln3": gain(ks[18], (DEPTH, D_MODEL)),
        "w_ple_gate": nrm(ks[19], (DEPTH, D_MODEL, D_MODEL), D_MODEL ** -0.5),
        "w_ple_proj": nrm(ks[20], (DEPTH, D_PLE, D_MODEL), D_PLE ** -0.5),
        "ple_norm": gain(ks[21], (DEPTH, D_MODEL)),
    }


def reference(x_prompt, x_sample, p_prompt, p_sample, cache_k, cache_v, page_table,
              ln1, w_in, a_v_norm, a_ws, a_bs, q_norm, k_norm, w_out,
              ln2, w_up, w_down, ln3, w_ple_gate, w_ple_proj, ple_norm):
    hp, hs = x_prompt, x_sample
    kp_l, vp_l, ks_l, vs_l, as_l = [], [], [], [], []
    for i in range(DEPTH):
        u, va, q, k, v = in_projection(rms_norm(hp, ln1[i]), w_in[i], a_v_norm[i], q_norm[i], k_norm[i])
        mix = jnp.concatenate([spatial_gate_prompt(u, va, a_ws[i], a_bs[i]), moba_prompt(q, k, v)], axis=-1)
        hp = channel_and_ple(hp + mix @ w_out[i], p_prompt[i], ln2[i], w_up[i], w_down[i],
                             ln3[i], w_ple_gate[i], w_ple_proj[i], ple_norm[i])
        kp_l.append(k)
        vp_l.append(v)
        u, va, q, k, v = in_projection(rms_norm(hs, ln1[i]), w_in[i], a_v_norm[i], q_norm[i], k_norm[i])
        mix = jnp.concatenate([spatial_gate_sample(u, va, a_ws[i], a_bs[i]),
                               moba_sample(q, k, v, cache_k[i], cache_v[i], page_table)], axis=-1)
        hs = channel_and_ple(hs + mix @ w_out[i], p_sample[i], ln2[i], w_up[i], w_down[i],
                             ln3[i], w_ple_gate[i], w_ple_proj[i], ple_norm[i])
        ks_l.append(k)
        vs_l.append(v)
        as_l.append(va)
    k_prompt_new = jnp.stack(kp_l)
    v_prompt_new = jnp.stack(vp_l)
    k_sample_new = jnp.stack(ks_l)
    v_sample_new = jnp.stack(vs_l)
    state_a_v_sample = jnp.stack(as_l)
    return (hp, hs, k_prompt_new, v_prompt_new, k_sample_new, v_sample_new, state_a_v_sample)
```

```python
from contextlib import ExitStack

import numpy as np
import concourse.bass as bass
import concourse.mybir as mybir
from concourse.bass_utils import run_bass_kernel_spmd

F32 = mybir.dt.float32
BF16 = mybir.dt.bfloat16
I32 = mybir.dt.int32
ALU = mybir.AluOpType
AF = mybir.ActivationFunctionType
AX = mybir.AxisListType

NCORES = 8
SEQ = 2048
NT = 16
NS = 16
NTOK = SEQ + NS
D = 1024
EPS = 1e-6
NEG = -30000.0
PHASES = 5
FLAGS = 511
SAMPLE = True
DBG = {}
PFD = 3
SKIP_SELF = False
ITEMS_PER_UNIT = 1
HEADF = (0.55, 0.5, 0.5)
NPOOL = 2560


class Sched:
    ENGS = ("pe", "act", "dve", "pool", "sp")

    def __init__(self, nc, stack):
        self.nc = nc
        self.prog = {e: [] for e in self.ENGS}
        self.esem = {e: stack.enter_context(nc.semaphore("s_" + e)) for e in ("pe", "act", "dve", "pool")}
        self.cnt = {e: 0 for e in self.esem}
        self.seen = {e: {} for e in self.ENGS}
        self.last_w = {}
        self.readers = {}
        self.dsem = {}
        self.dcnt = {}
        self.stack = stack
        self.final = []

    def _deps(self, reads, writes):
        deps = []
        for r in reads:
            t = self.last_w.get(r)
            if t is not None:
                deps.append(t)
        for w in writes:
            t = self.last_w.get(w)
            if t is not None:
                deps.append(t)
            deps.extend(self.readers.get(w, ()))
        return deps

    def _emit_waits(self, eng, deps):
        need = {}
        for (sem, val, src) in deps:
            if src == eng and (eng == "pe" or SKIP_SELF):
                continue
            k = id(sem)
            if self.seen[eng].get(k, 0) >= val:
                continue
            if k not in need or need[k][1] < val:
                need[k] = (sem, val)
        for k, (sem, val) in need.items():
            self.seen[eng][k] = val
            self.prog[eng].append(lambda e, sem=sem, val=val: e.wait_ge(sem, val))

    def _commit(self, tok, reads, writes):
        for w in writes:
            self.last_w[w] = tok
            self.readers[w] = []
        for r in reads:
            if r in writes:
                continue
            self.readers.setdefault(r, []).append(tok)

    _rec = None

    def record(self):
        self._rec = []

    def stop(self):
        r, self._rec = self._rec, None
        return r

    def replay(self, lst):
        for ent in lst:
            if ent[0] == "op":
                self.op(*ent[1:])
            else:
                self.dma(*ent[1:])

    @staticmethod
    def merge(a, b):
        out = []
        na, nb = len(a), len(b)
        ia = ib = 0
        while ia < na or ib < nb:
            if ib >= nb or (ia < na and ia * max(nb, 1) <= ib * max(na, 1)):
                out.append(a[ia]); ia += 1
            else:
                out.append(b[ib]); ib += 1
        return out

    def op(self, eng, fn, reads=(), writes=()):
        if self._rec is not None:
            self._rec.append(("op", eng, fn, tuple(reads), tuple(writes)))
            return None
        deps = self._deps(reads, writes)
        self._emit_waits(eng, deps)
        self.cnt[eng] += 1
        sem = self.esem[eng]
        tok = (sem, self.cnt[eng], eng)
        self.prog[eng].append(lambda e, fn=fn, sem=sem: fn(e).then_inc(sem, 1))
        self._commit(tok, reads, writes)
        return tok

    def dma(self, eng, fn, semkey, reads=(), writes=(), final=False):
        if self._rec is not None:
            self._rec.append(("dma", eng, fn, semkey, tuple(reads), tuple(writes), final))
            return None
        deps = self._deps(reads, writes)
        self._emit_waits(eng, deps)
        if semkey not in self.dsem:
            self.dsem[semkey] = self.stack.enter_context(self.nc.semaphore("d%d" % len(self.dsem)))
            self.dcnt[semkey] = 0
        self.dcnt[semkey] += 16
        sem = self.dsem[semkey]
        tok = (sem, self.dcnt[semkey], "dma")
        self.prog[eng].append(lambda e, fn=fn, sem=sem: fn(e).then_inc(sem, 16))
        self._commit(tok, reads, writes)
        if final:
            self.final.append(tok)
        return tok

    def barrier(self):
        toks = [(self.esem[e], self.cnt[e], e) for e in self.esem if self.cnt[e] > 0]
        toks += [(self.dsem[k], self.dcnt[k], "dma") for k in self.dsem]
        for eng in self.ENGS:
            self._emit_waits(eng, [(t[0], t[1], "bar") for t in toks])

    def run(self, block):
        self._emit_waits("sp", self.final)
        progs = self.prog

        @block.tensor
        def _(e):
            for f in progs["pe"]:
                f(e)

        @block.scalar
        def _(e):
            for f in progs["act"]:
                f(e)

        @block.vector
        def _(e):
            for f in progs["dve"]:
                f(e)

        @block.gpsimd
        def _(e):
            for f in progs["pool"]:
                f(e)

        @block.sync
        def _(e):
            for f in progs["sp"]:
                f(e)


def build_nc():
    nc = bass.Bass("TRN2", target_bir_lowering=False)
    di = lambda n, s, dt=F32: nc.dram_tensor(n, s, dt, kind="ExternalInput").ap()
    do = lambda n, s: nc.dram_tensor(n, s, F32, kind="ExternalOutput").ap()
    x = di("x", [NTOK, D])
    pin = di("p", [NTOK, 256])
    consts = di("consts", [128, 896])
    ln1 = di("ln1", [D]); ln2 = di("ln2", [D]); ln3 = di("ln3", [D])
    w_in = di("w_in", [D, 2560])
    a_v_norm = di("a_v_norm", [512])
    a_ws = di("a_ws", [4, 128, 128])
    a_bs = di("a_bs", [4, 128])
    q_norm = di("q_norm", [64]); k_norm = di("k_norm", [64])
    w_out = di("w_out", [D, D])
    w_up = di("w_up", [D, 4096]); w_down = di("w_down", [4096, D])
    w_pg = di("w_pg", [D, D]); w_pp = di("w_pp", [256, D])
    ple_norm = di("ple_norm", [D])
    ck = di("cache_k", [NPOOL * 64, 1024])
    cv = di("cache_v", [NPOOL * 64, 1024])
    pte = di("pt_even", [128], I32)
    pto = di("pt_odd", [128], I32)
    iota = di("iota64", [128, 1], F32)
    bmask_d = di("bmask", [32, 512])
    y = do("y", [NTOK, D])
    k_new = do("k_new", [NTOK, 512])
    v_new = do("v_new", [NTOK, 512])
    a_state = do("a_state", [NS, 512])

    B0 = 16512

    def at(name, shape, dt, kib):
        return nc.alloc_sbuf_tensor_at(name, shape, dt, offset=B0 + int(round(kib * 1024)))

    with ExitStack() as st:
        S = Sched(nc, st)
        pst = lambda name, shape, dt: st.enter_context(nc.psum_tensor(name, shape, dt))
        st.enter_context(nc.allow_non_contiguous_dma(reason="small param layouts"))

        cst = at("cst", [128, 896], F32, 204.25)
        idn = at("idn", [128, 128], BF16, 1.5)
        trib = at("trib", [128, 128], BF16, 1.75)
        gains = at("gains", [128, 1664], F32, 2.0)
        GQ, GK, GA, GP = gains[:, 0:64], gains[:, 64:128], gains[:, 128:640], gains[:, 640:1664]
        g123 = at("g123", [128, 24], F32, 8.5)
        bsT = at("bsT", [128, 4], F32, 8.625)
        bsS = at("bsS", [16, 4], F32, 8.65625)
        epsc = at("epsc", [128, 1], F32, 8.6875)
        onec = at("onec", [128, 2], BF16, 8.71875)
        onesb = onec[:, 1:2]
        onesf = at("onesf", [128, 64], F32, 8.75)
        ones128 = at("ones128", [1, 128], F32, 12.5)
        wsT = at("wsT", [128, 4, 128], BF16, 9.0)
        BD = at("BD", [16, 4, 16], BF16, 10.0)
        stt = [at("stt%d" % i, [128, 64], F32, 10.125 + 0.25 * i) for i in range(2)]
        kmT = at("kmT", [64, 8, 8], BF16, 10.625)
        QTs = at("QTs", [128, 8, NS], BF16, 10.75)
        KTs = at("KTs", [128, 8, NS], BF16, 11.0)
        oTs = at("oTs", [64, 8, NS], BF16, 11.25)
        vsb = at("vsb", [16, 512], BF16, 11.5)
        QT = at("QT", [128, 8, SEQ], BF16, 13.0)
        A_T = at("A_T", [128, 4, NTOK], BF16, 45.0)
        KT = at("KT", [128, 8, SEQ], BF16, 61.25)
        Vaug = at("Vaug", [128, NT, 8, 65], BF16, 93.25)
        win = at("win", [128, 8, 2560], BF16, 109.5)
        o = 149.5
        xt = [at("xt%d" % i, [128, D], F32, o + 4 * i) for i in range(2)]; o += 8
        junk = at("junk", [128, D], BF16, o); o += 2
        hn = [at("hn%d" % i, [128, D], BF16, o + 2 * i) for i in range(2)]; o += 4
        hnT = [at("hnT%d" % i, [128, 8, 128], BF16, o + 2 * i) for i in range(2)]; o += 4
        u_sb = [at("u%d" % i, [128, 512], BF16, o + i) for i in range(2)]; o += 2
        raw = [at("raw%d" % i, [128, 512], F32, o + 2 * i) for i in range(2)]; o += 4
        tmpa = [at("tmpa%d" % i, [128, 512], F32, o + 2 * i) for i in range(2)]; o += 4
        vanf = [at("vanf%d" % i, [128, 512], F32, o + 2 * i) for i in range(2)]; o += 4
        vab = [at("vab%d" % i, [128, 512], BF16, o + i) for i in range(2)]; o += 2
        kn = [at("kn%d" % i, [128, 512], F32, o + 2 * i) for i in range(2)]; o += 4
        vf = [at("vf%d" % i, [128, 512], F32, o + 2 * i) for i in range(2)]; o += 4
        Qst = [at("Qst%d" % i, [128, 8, 72], BF16, o + 1.125 * i) for i in range(2)]; o += 2.25
        Kst = [at("Kst%d" % i, [128, 8, 72], BF16, o + 1.125 * i) for i in range(2)]; o += 2.25
        Atok = [at("Atok%d" % i, [128, 512], BF16, o + i) for i in range(2)]; o += 2
        wsn = at("wsn", [128, 4, 128], F32, o); o += 2
        wsm = at("wsm", [128, 4, 128], BF16, o); o += 1
        kmh = at("kmh", [64, 8, 16], F32, o); o += 0.5
        gsb2 = [at("gsb%d" % i, [128, 8, 8], F32, o + 0.25 * i) for i in range(2)]; o += 0.5
        top82 = [at("top8%d" % i, [128, 8, 8], F32, o + 0.25 * i) for i in range(2)]; o += 0.5
        gtmp2 = [at("gtmp%d" % i, [128, 8, 8], F32, o + 0.25 * i) for i in range(2)]; o += 0.5
        oh = at("oh", [128, 8, 8, 8], BF16, o); o += 1
        assert o <= 204.25, o
        PTb = [at("PT%d" % i, [128, 512], BF16, 109.5 + i) for i in range(3)]
        o_sb = [at("osb%d" % i, [128, 256], F32, 112.5 + i) for i in range(2)]
        rc = [at("rc%d" % i, [128, 256], F32, 114.5 + i) for i in range(2)]
        o = 116.5
        ridx = at("ridx", [128, 128], I32, o); o += 0.5
        iot = at("iot", [128, 1], F32, o); o += 0.03125
        ridxf = at("ridxf", [128, 128], F32, o); o += 0.5
        BDQ = at("BDQ", [128, 4, 4, 8], BF16, o); o += 0.25
        Sown = at("Sown", [4, 32], F32, o); o += 0.125
        Pown = at("Pown", [4, 32], BF16, o); o += 0.09375
        kblk = [at("kblk%d" % i, [128, 2, 512], BF16, o + 2 * i) for i in range(3)]; o += 6
        vblk = [at("vblk%d" % i, [128, 2, 512], BF16, o + 2 * i) for i in range(3)]; o += 6
        ktp = [at("ktp%d" % i, [128, 4, 128], BF16, o + i) for i in range(4)]; o += 4
        S_all = at("S_all", [128, 64, 32], F32, o); o += 8
        PT_all = [at("PT_all%d" % i, [128, 64, 32], BF16, o + 4 * i) for i in range(2)]; o += 8
        gsum = at("gsum", [1, 2048], F32, o); o += 8
        gate_s = at("gate_s", [1, 1024], F32, o); o += 4
        top8s = at("top8s", [1, 32, 8], F32, o); o += 1
        biasr = at("biasr", [1, 1024], F32, o); o += 4
        vown = at("vown", [4, 4, 512], BF16, o); o += 4
        oc = at("oc", [32, 512], F32, o); o += 2
        bmask = at("bmask", [32, 512], F32, o); o += 2
        o64 = at("o64", [32, 64], F32, o); o += 0.25
        o64b = at("o64b", [32, 64], BF16, o); o += 0.125
        dens = at("dens", [32, 2], F32, o); o += 0.125
        assert o <= 183.75, o
        woA = at("woA", [128, 4, D], BF16, 183.75)
        woB = at("woB", [64, 8, D], BF16, 191.75)
        h1 = at("h1", [128, NT + 1, D], F32, 61.25)
        hn2T = at("hn2T", [128, 8, NTOK], BF16, 129.25)
        xt3 = [at("xt3_%d" % i, [128, D], F32, 161.5 + 4 * i) for i in range(2)]
        junk3 = at("junk3", [128, D], BF16, 169.5)
        hn3 = [at("hn3_%d" % i, [128, D], BF16, 171.5 + 2 * i) for i in range(2)]
        wu = [at("wu0", [128, 8, 1024], BF16, 13.0), at("wu1", [128, 8, 1024], BF16, 161.5)]
        wd = [at("wd0", [128, 8, 1024], BF16, 29.0), at("wd1", [128, 8, 1024], BF16, 177.5)]
        aT = [at("aT%d" % i, [128, 8, 512], BF16, 45.0 + 8 * i) for i in range(2)]
        rt = [at("rt%d" % i, [128, 512], BF16, 193.5 + i) for i in range(3)]
        wpg = at("wpg", [128, 8, D], BF16, 13.0)
        wpp = at("wpp", [128, 2, D], BF16, 29.0)
        o = 129.25
        pt5 = [at("pt5_%d" % i, [128, 256], F32, o + i) for i in range(2)]; o += 2
        pb5 = [at("pb5_%d" % i, [128, 256], BF16, o + 0.5 * i) for i in range(2)]; o += 1
        pT5 = [at("pT5_%d" % i, [128, 2, 128], BF16, o + 0.5 * i) for i in range(2)]; o += 1
        hn5 = [at("hn5_%d" % i, [128, D], BF16, o + 2 * i) for i in range(2)]; o += 4
        hnT5 = [at("hnT5_%d" % i, [128, 8, 128], BF16, o + 2 * i) for i in range(2)]; o += 4
        sig5 = [at("sig5_%d" % i, [128, D], F32, o + 4 * i) for i in range(2)]; o += 8
        e5 = [at("e5_%d" % i, [128, D], F32, o + 4 * i) for i in range(2)]; o += 8
        t5 = [at("t5_%d" % i, [128, D], F32, o + 4 * i) for i in range(2)]; o += 8
        y5 = [at("y5_%d" % i, [128, D], F32, o + 4 * i) for i in range(2)]; o += 8
        junk5 = at("junk5", [128, D], BF16, o); o += 2
        assert o <= 183.75, o

        DBG.update(dict(QTs=QTs, KTs=KTs, PT1=PT_all[1], oTs=oTs, gate_s=gate_s, top8s=top8s, biasr=biasr, dens=dens, o64=o64, vown=vown, Pown=Pown, ridx=ridx, kblk0=kblk[0], S_all=S_all, BDQ=BDQ, vblk0=vblk[0], vblk1=vblk[1], vblk2=vblk[2], oc=oc, bmask=bmask))

        def finish():
            block = st.enter_context(nc.Block())
            S.run(block)
            return nc

        psT = [pst("psT%d" % i, [128, 1024], BF16) for i in range(2)]
        psZ = [pst("psZ%d" % i, [128, 512], F32) for i in range(4)]
        psA = [pst("psA%d" % i, [128, 512], F32) for i in range(2)]
        tcount = [0]

        cur_par = [None]

        def tbank():
            if cur_par[0] is not None:
                return cur_par[0]
            tcount[0] += 1
            return tcount[0] % 2

        ZB = [[(psZ[0], "psZ0"), (psZ[1], "psZ1"), (psA[0], "psA0")], [(psZ[2], "psZ2"), (psZ[3], "psZ3"), (psA[1], "psA1")]]

        S.dma("sp", lambda e: e.dma_start(out=cst[:], in_=consts[:, :]), "cst", writes=["cst"])
        S.dma("sp", lambda e: e.dma_start(out=gains[:, 0:64], in_=q_norm.partition_broadcast(128)), "gq", writes=["gq"])
        S.dma("sp", lambda e: e.dma_start(out=gains[:, 64:128], in_=k_norm.partition_broadcast(128)), "gk", writes=["gk"])
        S.dma("sp", lambda e: e.dma_start(out=gains[:, 128:640], in_=a_v_norm.partition_broadcast(128)), "ga", writes=["ga"])
        S.dma("sp", lambda e: e.dma_start(out=gains[:, 640:1664], in_=ple_norm.partition_broadcast(128)), "gp", writes=["gp"])
        for i, l in enumerate((ln1, ln2, ln3)):
            S.dma("sp", lambda e, i=i, l=l: e.dma_start(out=g123[:, 8 * i:8 * i + 8], in_=l.rearrange("(k p) -> p k", p=128)),
                  "g%d" % i, writes=["g123.%d" % i])
        S.dma("sp", lambda e: e.dma_start(out=bsT[:], in_=a_bs.rearrange("g t -> t g")), "bsT", writes=["bsT"])
        for b in range(4):
            S.dma("sp", lambda e, b=b: e.dma_start(out=bsS[4 * b:4 * b + 4, :], in_=a_bs[:, 0:4].rearrange("g t -> t g")),
                  "bsS", writes=["bsS"])
        S.dma("sp", lambda e: e.dma_start(out=wsn[:], in_=a_ws.rearrange("g t s -> t g s")), "wsn", writes=["wsn"])
        for kc in range(8):
            S.dma("pool", lambda e, kc=kc: e.dma_start(out=win[:, kc, :], in_=w_in[kc * 128:(kc + 1) * 128, :]),
                  "win%d" % kc, writes=["win%d" % kc])
        S.op("dve", lambda e: e.tensor_copy(out=idn[:], in_=cst[:, 0:128]), reads=["cst"], writes=["idn"])
        S.op("dve", lambda e: e.tensor_copy(out=trib[:], in_=cst[:, 256:384]), reads=["cst"], writes=["trib"])
        S.op("dve", lambda e: e.memset(epsc[:], EPS), writes=["epsc"])
        S.op("dve", lambda e: e.memset(onec[:, 0:1], 1.0 / 256), writes=["onec"])
        S.op("dve", lambda e: e.memset(onesf[:], 1.0), writes=["onesf"])
        S.op("dve", lambda e: e.memset(ones128[:], 1.0), writes=["ones128"])
        S.op("dve", lambda e: e.memset(onec[:, 1:2], 1.0), writes=["onesb"])
        for i in range(2):
            S.op("dve", lambda e, i=i: e.memset(gsb2[i][:], -3.0e38), writes=["gsb%d" % i])
        S.op("dve", lambda e: e.tensor_copy(out=oh[:].rearrange("p a b c -> p (a b c)"), in_=cst[:, 384:896]), reads=["cst"], writes=["oh"])
        for i in range(2):
            S.op("dve", lambda e, i=i: e.memset(Kst[i][:], 0.0), writes=["Kst%d.a" % i, "Kst%d.b" % i])
        if FLAGS & 32:
            S.op("pool", lambda e: e.memset(Vaug[:], 1.0), writes=["Vaug"])
        for i in range(2):
            S.op("pool", lambda e, i=i: e.memset(Qst[i][:, :, 64:72], 0.0), writes=["Qst%d.b" % i])
        S.op("dve", lambda e: e.tensor_scalar(out=gains[:, 0:64], in0=gains[:, 0:64], scalar1=0.125, scalar2=None, op0=ALU.mult),
             reads=["gq"], writes=["gq"])
        S.op("dve", lambda e: e.tensor_tensor(out=wsm[:], in0=wsn[:], in1=cst[:, 128:256].unsqueeze(1).to_broadcast([128, 4, 128]), op=ALU.mult),
             reads=["wsn", "cst"], writes=["wsm"])
        tb = tbank()
        for g in range(4):
            S.op("pe", lambda e, g=g, tb=tb: e.transpose(out=psT[tb][:, g * 128:(g + 1) * 128], in_=wsm[:, g, :], identity=idn[:]),
                 reads=["wsm", "idn"], writes=["psT%d" % tb])
        S.op("act", lambda e, tb=tb: e.activation(out=wsT[:], in_=psT[tb][:, 0:512].rearrange("p (g t) -> p g t", g=4), func=AF.Copy),
             reads=["psT%d" % tb], writes=["wsT"])
        S.op("dve", lambda e: e.memset(BD[:], 0.0), writes=["BD"])
        for b in range(4):
            S.dma("sp", lambda e, b=b: e.dma_start(out=BD[4 * b:4 * b + 4, :, 4 * b:4 * b + 4], in_=wsT[0:4, :, 0:4]),
                  "BD", reads=["wsT"], writes=["BD"])

        def rms_rstd(ST, R, ks, src, kin, junk_t, kjunk, n):
            S.op("act", lambda e: e.activation(out=junk_t[0:R, :], in_=src, func=AF.Square, accum_out=ST[0:R, 0:1]),
                 reads=[kin], writes=[kjunk, ks + ".0"])
            S.op("act", lambda e: e.activation(out=ST[0:R, 1:2], in_=ST[0:R, 0:1], func=AF.Sqrt, scale=1.0 / n, bias=epsc[0:R, :]),
                 reads=[ks + ".0", "epsc"], writes=[ks + ".1"])
            S.op("dve", lambda e: e.reciprocal(out=ST[0:R, 2:3], in_=ST[0:R, 1:2]), reads=[ks + ".1"], writes=[ks + ".2"])

        def norm_T(HN, khn, R, goff, dst, kdst):
            tb = tbank()
            for kc in range(8):
                S.op("pe", lambda e, kc=kc: e.transpose(out=psT[tb][:, kc * 128:kc * 128 + R], in_=HN[0:R, kc * 128:(kc + 1) * 128],
                                                        identity=idn[0:R, 0:R]),
                     reads=[khn, "idn"], writes=["psT%d" % tb])
            S.op("dve", lambda e: e.tensor_tensor(out=dst, in0=psT[tb][:, :].rearrange("p (k t) -> p k t", k=8)[:, :, 0:R],
                                                  in1=g123[:, goff:goff + 8].unsqueeze(2).to_broadcast([128, 8, R]), op=ALU.mult),
                 reads=["psT%d" % tb, "g123.%d" % (goff // 8)], writes=[kdst])

        h8 = lambda ap: ap.rearrange("p (h d) -> p h d", h=8)
        g4 = lambda ap: ap.rearrange("p (g w) -> p g w", g=4)

        def q_transposes(j, s, R, r0):
            tb = tbank()
            for h in range(8):
                S.op("pe", lambda e, h=h: e.transpose(out=psT[tb][0:72, h * 128:h * 128 + R], in_=Qst[s][0:R, h, 0:72], identity=idn[0:R, 0:R]),
                     reads=["Qst%d.a" % s, "Qst%d.b" % s, "idn"], writes=["psT%d" % tb])
            src = psT[tb][0:72, :].rearrange("p (h t) -> p h t", h=8)[:, :, 0:R]
            if j < NT:
                S.op("act", lambda e: e.activation(out=QT[0:72, :, r0:r0 + R], in_=src, func=AF.Copy),
                     reads=["psT%d" % tb], writes=["QT.%d" % j])
            else:
                S.op("act", lambda e: e.activation(out=QTs[0:72, :, 0:R], in_=src, func=AF.Copy), reads=["psT%d" % tb], writes=["QTs"])


        def pipelined(fn, n, frac):
            lists = []
            for j in range(n):
                S.record()
                fn(j)
                lists.append(S.stop())
            if frac <= 0:
                for l in lists:
                    S.replay(l)
                return
            cut = [int(len(l) * frac) for l in lists]
            S.replay(lists[0][:cut[0]])
            for j in range(n):
                tail = lists[j][cut[j]:]
                nxt = lists[j + 1][:cut[j + 1]] if j + 1 < n else []
                S.replay(Sched.merge(tail, nxt))

        def p1_tile(j):
            R = 128 if j < NT else NS
            r0 = 128 * j
            s = j % 2
            XT, ST, HN, HNT = xt[s], stt[s], hn[s], hnT[s]
            kx, ks = "xt%d" % s, "st%d" % s
            cur_par[0] = s
            gsb, top8, gtmp = gsb2[s], top82[s], gtmp2[s]
            kg = "gsb%d" % s
            S.dma("sp", lambda e: e.dma_start(out=XT[0:R, :], in_=x[r0:r0 + R, :]), kx, writes=[kx])
            rms_rstd(ST, R, ks, XT[0:R, :], kx, junk, "junk", D)
            S.op("pool", lambda e: e.tensor_scalar(out=HN[0:R, :], in0=XT[0:R, :], scalar1=ST[0:R, 2:3], scalar2=None, op0=ALU.mult),
                 reads=[kx, ks + ".2"], writes=["hn%d" % s])
            norm_T(HN, "hn%d" % s, R, 0, HNT[:, :, 0:R], "hnT%d" % s)
            zc = [0]

            def zbank():
                zc[0] += 1
                return ZB[s][zc[0] % 3]

            for c in range(5):
                Z, kz = zbank()
                for kc in range(8):
                    S.op("pe", lambda e, kc=kc, c=c, Z=Z: e.matmul(Z[0:R, :], lhsT=HNT[:, kc, 0:R],
                                                                   rhs=win[:, kc, c * 512:(c + 1) * 512], start=(kc == 0), stop=(kc == 7)),
                         reads=["hnT%d" % s, "win%d" % kc], writes=[kz])
                if c == 0:
                    S.op("act", lambda e, Z=Z: e.activation(out=u_sb[s][0:R, :], in_=Z[0:R, :], func=AF.Gelu_apprx_tanh),
                         reads=[kz], writes=["u%d" % s])
                elif c == 1:
                    S.op("act", lambda e, Z=Z: e.activation(out=raw[s][0:R, :], in_=Z[0:R, :], func=AF.Gelu_apprx_tanh),
                         reads=[kz], writes=["raw%d" % s])
                    S.op("dve", lambda e: e.tensor_tensor(out=tmpa[s][0:R, :], in0=raw[s][0:R, :], in1=raw[s][0:R, :], op=ALU.mult),
                         reads=["raw%d" % s], writes=["tmpa%d" % s])
                    S.op("dve", lambda e: e.tensor_reduce(out=ST[0:R, 4:8], in_=g4(tmpa[s][0:R, :]), axis=AX.X, op=ALU.add),
                         reads=["tmpa%d" % s], writes=[ks + ".a"])
                    S.op("act", lambda e: e.activation(out=ST[0:R, 8:12], in_=ST[0:R, 4:8], func=AF.Sqrt, scale=1.0 / 128, bias=epsc[0:R, :]),
                         reads=[ks + ".a", "epsc"], writes=[ks + ".b"])
                    S.op("dve", lambda e: e.reciprocal(out=ST[0:R, 12:16], in_=ST[0:R, 8:12]), reads=[ks + ".b"], writes=[ks + ".c"])
                    S.op("dve", lambda e: e.tensor_tensor(out=g4(tmpa[s][0:R, :]), in0=g4(raw[s][0:R, :]),
                                                          in1=ST[0:R, 12:16].unsqueeze(2).to_broadcast([R, 4, 128]), op=ALU.mult),
                         reads=["raw%d" % s, ks + ".c"], writes=["tmpa%d" % s])
                    S.op("dve", lambda e: e.tensor_tensor(out=vanf[s][0:R, :], in0=tmpa[s][0:R, :], in1=GA[0:R, :], op=ALU.mult),
                         reads=["tmpa%d" % s, "ga"], writes=["vanf%d" % s])
                    S.op("pool", lambda e: e.tensor_copy(out=vab[s][0:R, :], in_=vanf[s][0:R, :]),
                         reads=["vanf%d" % s], writes=["vab%d" % s])
                    if j == NT:
                        S.dma("sp", lambda e: e.dma_start(out=a_state[:, :], in_=vanf[s][0:R, :]), "o_vanf%d" % s,
                              reads=["vanf%d" % s], final=True)
                elif c in (2, 3):
                    o0 = 16 if c == 2 else 40
                    S.op("act", lambda e, Z=Z: e.activation(out=raw[s][0:R, :], in_=Z[0:R, :], func=AF.Copy), reads=[kz], writes=["raw%d" % s])
                    S.op("act", lambda e, Z=Z: e.activation(out=tmpa[s][0:R, :], in_=Z[0:R, :], func=AF.Square), reads=[kz], writes=["tmpa%d" % s])
                    S.op("dve", lambda e, o0=o0: e.tensor_reduce(out=ST[0:R, o0:o0 + 8], in_=h8(tmpa[s][0:R, :]), axis=AX.X, op=ALU.add),
                         reads=["tmpa%d" % s], writes=[ks + ".q%d" % c])
                    S.op("act", lambda e, o0=o0: e.activation(out=ST[0:R, o0 + 8:o0 + 16], in_=ST[0:R, o0:o0 + 8], func=AF.Sqrt, scale=1.0 / 64,
                                                              bias=epsc[0:R, :]),
                         reads=[ks + ".q%d" % c, "epsc"], writes=[ks + ".r%d" % c])
                    S.op("dve", lambda e, o0=o0: e.reciprocal(out=ST[0:R, o0 + 16:o0 + 24], in_=ST[0:R, o0 + 8:o0 + 16]),
                         reads=[ks + ".r%d" % c], writes=[ks + ".s%d" % c])
                    S.op("dve", lambda e, o0=o0: e.tensor_tensor(out=h8(tmpa[s][0:R, :]), in0=h8(raw[s][0:R, :]),
                                                                 in1=ST[0:R, o0 + 16:o0 + 24].unsqueeze(2).to_broadcast([R, 8, 64]), op=ALU.mult),
                         reads=["raw%d" % s, ks + ".s%d" % c], writes=["tmpa%d" % s])
                    if c == 2:
                        S.op("dve", lambda e: e.tensor_tensor(out=Qst[s][0:R, :, 0:64], in0=h8(tmpa[s][0:R, :]),
                                                              in1=GQ[0:R, :].unsqueeze(1).to_broadcast([R, 8, 64]), op=ALU.mult),
                             reads=["tmpa%d" % s, "gq"], writes=["Qst%d.a" % s])
                    else:
                        S.op("dve", lambda e: e.tensor_tensor(out=h8(kn[s][0:R, :]), in0=h8(tmpa[s][0:R, :]),
                                                              in1=GK[0:R, :].unsqueeze(1).to_broadcast([R, 8, 64]), op=ALU.mult),
                             reads=["tmpa%d" % s, "gk"], writes=["kn%d" % s])
                        S.dma("sp", lambda e: e.dma_start(out=k_new[r0:r0 + R, :], in_=kn[s][0:R, :]), "o_kn%d" % s,
                              reads=["kn%d" % s], final=True)
                        S.op("pool", lambda e: e.tensor_copy(out=Kst[s][0:R, :, 0:64], in_=h8(kn[s][0:R, :])),
                             reads=["kn%d" % s], writes=["Kst%d.a" % s])
                else:
                    S.op("act", lambda e, Z=Z: e.activation(out=vf[s][0:R, :], in_=Z[0:R, :], func=AF.Copy), reads=[kz], writes=["vf%d" % s])
                    S.dma("sp", lambda e: e.dma_start(out=v_new[r0:r0 + R, :], in_=vf[s][0:R, :]), "o_vf%d" % s,
                          reads=["vf%d" % s], final=True)
                    if j == NT:
                        S.op("act", lambda e: e.activation(out=vsb[0:R, :], in_=vf[s][0:R, :], func=AF.Copy), reads=["vf%d" % s], writes=["vsb"])
                    if j < NT and (FLAGS & 16):
                        S.op("pool", lambda e: e.tensor_copy(out=Vaug[:, j, :, 0:64], in_=h8(vf[s][:, :])),
                             reads=["vf%d" % s], writes=["Vaug"])
            if not (FLAGS & 1):
                return
            Z, kz = zbank()
            for g in range(4):
                lhs = wsT[:, g, :] if j < NT else BD[0:NS, g, :]
                S.op("pe", lambda e, g=g, lhs=lhs, Z=Z: e.matmul(Z[0:R, g * 128:(g + 1) * 128], lhsT=lhs, rhs=vab[s][0:R, g * 128:(g + 1) * 128],
                                                                 start=True, stop=True),
                     reads=["wsT", "BD", "vab%d" % s], writes=[kz])
            bs_t = bsT if j < NT else bsS
            for g in range(4):
                S.op("dve", lambda e, g=g, Z=Z: e.scalar_tensor_tensor(out=Atok[s][0:R, g * 128:(g + 1) * 128], in0=Z[0:R, g * 128:(g + 1) * 128],
                                                                       scalar=bs_t[0:R, g:g + 1], in1=u_sb[s][0:R, g * 128:(g + 1) * 128],
                                                                       op0=ALU.add, op1=ALU.mult),
                     reads=[kz, "bsT", "bsS", "u%d" % s], writes=["Atok%d" % s])
            tba = tbank()
            for g in range(4):
                S.op("pe", lambda e, g=g: e.transpose(out=psT[tba][:, g * 128:g * 128 + R], in_=Atok[s][0:R, g * 128:(g + 1) * 128],
                                                      identity=idn[0:R, 0:R]),
                     reads=["Atok%d" % s, "idn"], writes=["psT%d" % tba])
            S.op("act", lambda e: e.activation(out=A_T[:, :, r0:r0 + R], in_=psT[tba][:, 0:512].rearrange("p (g t) -> p g t", g=4)[:, :, 0:R],
                                               func=AF.Copy),
                 reads=["psT%d" % tba], writes=["A_T.%d" % j])
            if not (FLAGS & 2):
                return
            if j == NT and not (FLAGS & 256):
                return
            if j < NT and (FLAGS & 512):
                return
            ob = j // 2
            if j < NT and (FLAGS & 64):
                S.op("act", lambda e: e.activation(out=Kst[s][:, :, 64:72], in_=oh[:, ob, :, :], func=AF.Copy), reads=["oh"], writes=["Kst%d.b" % s])
            tb = tbank()
            KW = 72 if j < NT else 64
            for h in range(8):
                S.op("pe", lambda e, h=h: e.transpose(out=psT[tb][0:KW, h * 128:h * 128 + R], in_=Kst[s][0:R, h, 0:KW], identity=idn[0:R, 0:R]),
                     reads=["Kst%d.a" % s, "Kst%d.b" % s, "idn"], writes=["psT%d" % tb])
            src = psT[tb][0:KW, :].rearrange("p (h t) -> p h t", h=8)[:, :, 0:R]
            if j < NT:
                S.op("dve", lambda e: e.tensor_copy(out=KT[0:72, :, r0:r0 + R], in_=src), reads=["psT%d" % tb], writes=["KT.%d" % j])
                if not (FLAGS & 128):
                    return
                Z2, kz2 = zbank()
                for h in range(8):
                    S.op("pe", lambda e, h=h, Z2=Z2: e.matmul(Z2[0:64, h:h + 1], lhsT=Kst[s][:, h, 0:64], rhs=onec[:, 0:1], start=True, stop=True),
                         reads=["Kst%d.a" % s, "onec"], writes=[kz2])
                S.op("dve", lambda e, Z2=Z2: e.tensor_copy(out=kmh[0:64, :, j], in_=Z2[0:64, 0:8]), reads=[kz2], writes=["kmh.%d" % j])
                if j % 2 == 1:
                    S.op("dve", lambda e: e.tensor_tensor(out=kmT[0:64, :, ob], in0=kmh[0:64, :, j - 1], in1=kmh[0:64, :, j], op=ALU.add),
                         reads=["kmh.%d" % (j - 1), "kmh.%d" % j], writes=["kmT.%d" % ob])
            else:
                S.op("dve", lambda e: e.tensor_copy(out=KTs[0:64, :, 0:R], in_=src), reads=["psT%d" % tb], writes=["KTs"])
            if not (FLAGS & 4):
                return
            q_transposes(j, s, R, r0)
            if j < NT and ob >= 4 and (FLAGS & 8):
                Z3, kz3 = zbank()
                for h in range(8):
                    S.op("pe", lambda e, h=h, Z3=Z3: e.matmul(Z3[:, h * 8:h * 8 + ob], lhsT=QT[0:64, h, r0:r0 + 128], rhs=kmT[0:64, h, 0:ob],
                                                              start=True, stop=True),
                         reads=["QT.%d" % j] + ["kmT.%d" % n for n in range(ob)], writes=[kz3])
                S.op("dve", lambda e, Z3=Z3: e.tensor_copy(out=gsb[:, :, 0:ob], in_=Z3[:, 0:64].rearrange("p (h n) -> p h n", h=8)[:, :, 0:ob]),
                     reads=[kz3], writes=[kg])
                for h in range(8):
                    S.op("dve", lambda e, h=h: e.max(out=top8[:, h, :], in_=gsb[:, h, :]), reads=[kg], writes=[kg + ".t%d" % h])
                S.op("dve", lambda e: e.tensor_tensor(out=gtmp[:, :, 0:ob], in0=gsb[:, :, 0:ob], in1=top8[:, :, 2:3].to_broadcast([128, 8, ob]),
                                                      op=ALU.is_lt),
                     reads=[kg] + [kg + ".t%d" % h for h in range(8)], writes=[kg + ".g"])
                S.op("dve", lambda e: e.tensor_scalar(out=Qst[s][:, :, 64:64 + ob], in0=gtmp[:, :, 0:ob], scalar1=NEG, scalar2=None, op0=ALU.mult),
                     reads=[kg + ".g"], writes=["Qst%d.b" % s])
                q_transposes(j, s, R, r0)

        pipelined(p1_tile, NT + 1, HEADF[0])
        cur_par[0] = None
        if PHASES <= 1:
            return finish()
        S.barrier()
        S.dma("pool", lambda e: e.dma_start(out=woA[:], in_=w_out[0:512, :].rearrange("(g p) f -> p g f", p=128)), "woA", writes=["woA"])
        S.dma("sp", lambda e: e.dma_start(out=ridx[0:64, :], in_=pte.partition_broadcast(64)), "ridxa", writes=["ridx.a"])
        S.dma("sp", lambda e: e.dma_start(out=ridx[64:128, :], in_=pto.partition_broadcast(64)), "ridxb", writes=["ridx.b"])
        S.dma("sp", lambda e: e.dma_start(out=iot[:], in_=iota[:, :]), "iot", writes=["iot"])
        S.dma("sp", lambda e: e.dma_start(out=bmask[:], in_=bmask_d[:, :]), "bmask", writes=["bmask"])
        S.op("dve", lambda e: e.tensor_copy(out=ridxf[:], in_=ridx[:]), reads=["ridx.a", "ridx.b"], writes=["ridxf"])
        S.op("dve", lambda e: e.tensor_scalar(out=ridxf[:], in0=ridxf[:], scalar1=64.0, scalar2=iot[:, 0:1], op0=ALU.mult, op1=ALU.add),
             reads=["ridxf", "iot"], writes=["ridxf"])
        S.op("dve", lambda e: e.tensor_copy(out=ridx[:], in_=ridxf[:]), reads=["ridxf"], writes=["ridx"])
        S.op("dve", lambda e: e.memset(BDQ[:], 0.0), writes=["BDQ"])
        QTs4 = QTs[0:64, :, :].rearrange("p (hp two) t -> p hp two t", two=2)
        for b in range(4):
            S.op("dve", lambda e, b=b: e.tensor_copy(out=BDQ[0:64, b, :, 0:4], in_=QTs4[:, :, 0, 4 * b:4 * b + 4]), reads=["QTs", "BDQ"], writes=["BDQ"])
            S.dma("sp", lambda e, b=b: e.dma_start(out=BDQ[64:128, b, :, 4:8], in_=QTs4[:, :, 1, 4 * b:4 * b + 4]), "BDQ", reads=["QTs"], writes=["BDQ"])
            S.dma("sp", lambda e, b=b: e.dma_start(out=vown[0:4, b, :], in_=vsb[4 * b:4 * b + 4, :]), "vown", reads=["vsb"], writes=["vown"])

        gq_ = []

        def gather(kind, b, n, slot):
            dst = (kblk if kind == "k" else vblk)[slot]
            src = ck if kind == "k" else cv
            key = "%sblk%d" % (kind, slot)
            col = b * 32 + n
            S.dma("pool", lambda e: e.indirect_dma_start(out=dst[:].rearrange("p r f -> p (r f)"), out_offset=None, in_=src[:, :],
                                                         in_offset=bass.IndirectOffsetOnAxis(ap=ridx[:, col:col + 1], axis=0)),
                  key, reads=["ridx"], writes=[key])

        kcnt = [0]; vcnt = [0]; tpc = [0]

        pendK = []

        def flushK():
            while pendK:
                pendK.pop(0)()

        def item_K(b, n):
            i = kcnt[0]; kcnt[0] += 1
            slot = i % 3
            KB = kblk[slot]; kk = "kblk%d" % slot
            kps = []
            for r in range(2):
                t = tpc[0]; tpc[0] += 1
                tb = t % 2
                KP = ktp[t % 4]; kkp = "ktp%d" % (t % 4)
                kps.append((KP, kkp))
                for hp in range(4):
                    S.op("pe", lambda e, hp=hp, r=r, tb=tb: e.transpose(out=psT[tb][:, hp * 128:(hp + 1) * 128], in_=KB[:, r, hp * 128:(hp + 1) * 128],
                                                                     identity=idn[:]), reads=[kk, "idn"], writes=["psT%d" % tb])
                if t % 2 == 0:
                    S.op("act", lambda e, tb=tb, KP=KP: e.activation(out=KP[:].rearrange("p h t -> p (h t)"), in_=psT[tb][:, 0:512], func=AF.Copy),
                         reads=["psT%d" % tb], writes=[kkp])
                else:
                    S.op("dve", lambda e, tb=tb, KP=KP: e.tensor_copy(out=KP[:].rearrange("p h t -> p (h t)"), in_=psT[tb][:, 0:512]),
                         reads=["psT%d" % tb], writes=[kkp])

            def M():
                for r in range(2):
                    KP, kkp = kps[r]
                    for hp in range(4):
                        S.op("pe", lambda e, hp=hp, r=r, KP=KP: e.matmul(psZ[2][:, 256 + r * 32 + hp * 8:256 + r * 32 + hp * 8 + 8], lhsT=KP[:, hp, :], rhs=BDQ[:, b, hp, :],
                                                                         start=True, stop=True), reads=[kkp, "BDQ"], writes=["psZ2"])
                S.op("dve", lambda e: e.tensor_copy(out=S_all[:, 2 * n:2 * n + 2, :], in_=psZ[2][:, 256:320].rearrange("p (r c) -> p r c", r=2)),
                     reads=["psZ2"], writes=["S_all"])

            flushK()
            pendK.append(M)

        def item_GATE(b):
            pi = b % 2
            PTA = PT_all[pi]; kpta = "PT_all%d" % pi
            Sf = S_all[:].rearrange("p g c -> p (g c)")
            for q4 in range(4):
                S.op("pe", lambda e, q4=q4: e.matmul(psZ[2][0:1, 0:512], lhsT=onesf[:, 0:1], rhs=Sf[:, q4 * 512:(q4 + 1) * 512], start=True, stop=True),
                     reads=["S_all", "onesf"], writes=["psZ2"])
                S.op("act", lambda e, q4=q4: e.activation(out=gsum[0:1, q4 * 512:(q4 + 1) * 512], in_=psZ[2][0:1, 0:512], func=AF.Copy),
                     reads=["psZ2"], writes=["gsum"])
            gs4 = gsum[0:1, :].rearrange("o (n j c) -> o n j c", j=2, c=32)
            S.op("dve", lambda e: e.tensor_tensor(out=gate_s[0:1, :].rearrange("o (n c) -> o n c", c=32), in0=gs4[:, :, 0, :], in1=gs4[:, :, 1, :], op=ALU.add),
                 reads=["gsum"], writes=["gate_s"])
            gcn = gate_s[0:1, :].rearrange("o (n c) -> o c n", c=32)
            for c in range(32):
                S.op("dve", lambda e, c=c: e.max(out=top8s[0:1, c, :], in_=gcn[:, c, :]), reads=["gate_s"], writes=["top8s.%d" % c])
            S.op("dve", lambda e: e.tensor_tensor(out=biasr[0:1, :].rearrange("o (n c) -> o n c", c=32), in0=gate_s[0:1, :].rearrange("o (n c) -> o n c", c=32),
                                                  in1=top8s[0:1, :, 2:3].rearrange("o c k -> o k c").to_broadcast([1, 32, 32]), op=ALU.is_lt),
                 reads=["gate_s"] + ["top8s.%d" % c for c in range(32)], writes=["biasr"])
            S.op("dve", lambda e: e.tensor_scalar(out=biasr[0:1, :], in0=biasr[0:1, :], scalar1=NEG, scalar2=None, op0=ALU.mult), reads=["biasr"], writes=["biasr"])
            S4 = S_all[:].rearrange("p (n j) c -> p n j c", j=2)
            for q4 in range(4):
                S.op("pe", lambda e, q4=q4: e.matmul(psZ[2][:, 0:256], lhsT=ones128[0:1, :], rhs=biasr[0:1, q4 * 256:(q4 + 1) * 256], start=True, stop=True),
                     reads=["biasr", "ones128"], writes=["psZ2"])
                S.op("dve", lambda e, q4=q4: e.tensor_tensor(out=S4[:, q4 * 8:(q4 + 1) * 8, :, :], in0=S4[:, q4 * 8:(q4 + 1) * 8, :, :],
                                                             in1=psZ[2][:, 0:256].rearrange("p (n c) -> p n c", c=32).unsqueeze(2).to_broadcast([128, 8, 2, 32]), op=ALU.add),
                     reads=["psZ2", "S_all"], writes=["S_all"])
            S.op("act", lambda e: e.activation(out=PTA[:].rearrange("p g c -> p (g c)"), in_=Sf, func=AF.Exp), reads=["S_all"], writes=[kpta])
            for h in range(8):
                S.op("pe", lambda e, h=h: e.matmul(psZ[2][0:4, 328 + 4 * h:332 + 4 * h], lhsT=KTs[0:64, h, 4 * b:4 * b + 4], rhs=QTs[0:64, h, 4 * b:4 * b + 4],
                                                   start=True, stop=True), reads=["KTs", "QTs"], writes=["psZ2"])
            S.op("dve", lambda e: e.tensor_tensor(out=Sown[:].rearrange("k (h q) -> k h q", h=8), in0=psZ[2][0:4, 328:360].rearrange("k (h q) -> k h q", h=8),
                                                  in1=trib[0:4, 0:4].unsqueeze(1).to_broadcast([4, 8, 4]), op=ALU.add),
                 reads=["psZ2", "trib"], writes=["Sown"])
            S.op("act", lambda e: e.activation(out=Pown[:], in_=Sown[:], func=AF.Exp), reads=["Sown"], writes=["Pown"])

        def item_V(b, n):
            i = vcnt[0]; vcnt[0] += 1
            slot = i % 3
            VB = vblk[slot]; kv = "vblk%d" % slot
            PTA = PT_all[b % 2]; kpta = "PT_all%d" % (b % 2)
            for r in range(2):
                S.op("pe", lambda e, r=r: e.matmul(psZ[3][0:32, :], lhsT=PTA[:, 2 * n + r, :], rhs=VB[:, r, :], start=(n == 0 and r == 0), stop=False),
                     reads=[kv, kpta], writes=["psZ3"])

        def item_FIN(b):
            PTA = PT_all[b % 2]; kpta = "PT_all%d" % (b % 2)
            S.op("pe", lambda e: e.matmul(psZ[3][0:32, :], lhsT=Pown[:, :], rhs=vown[0:4, b, :], start=False, stop=True),
                 reads=["Pown", "vown"], writes=["psZ3"])
            for g in range(64):
                S.op("pe", lambda e, g=g: e.matmul(psZ[2][0:32, 320:321], lhsT=PTA[:, g, :], rhs=onec[:, 1:2], start=(g == 0), stop=False),
                     reads=[kpta, "onesb"], writes=["psZ2"])
            S.op("pe", lambda e: e.matmul(psZ[2][0:32, 320:321], lhsT=Pown[:, :], rhs=onec[0:4, 1:2], start=False, stop=True),
                 reads=["Pown", "onesb"], writes=["psZ2"])
            S.op("dve", lambda e: e.reciprocal(out=dens[:, 0:1], in_=psZ[2][0:32, 320:321]), reads=["psZ2"], writes=["dens"])
            S.op("dve", lambda e: e.tensor_tensor(out=oc[:], in0=psZ[3][0:32, :], in1=bmask[:], op=ALU.mult), reads=["psZ3", "bmask"], writes=["oc"])
            S.op("dve", lambda e: e.tensor_reduce(out=o64[:], in_=oc[:].rearrange("c (h d) -> c d h", h=8), axis=AX.X, op=ALU.add), reads=["oc"], writes=["o64"])
            S.op("dve", lambda e: e.tensor_scalar(out=o64b[:], in0=o64[:], scalar1=dens[:, 0:1], scalar2=None, op0=ALU.mult), reads=["o64", "dens"], writes=["o64b"])
            S.op("pe", lambda e: e.transpose(out=psT[0][0:64, 0:32], in_=o64b[:], identity=idn[0:32, 0:32]), reads=["o64b", "idn"], writes=["psT0"])
            S.op("act", lambda e: e.activation(out=oTs[0:64, :, 4 * b:4 * b + 4], in_=psT[0][0:64, 0:32].rearrange("d (h q) -> d h q", h=8), func=AF.Copy),
                 reads=["psT0"], writes=["oTs"])

        items = []
        for b in range(4):
            for n in range(32):
                items.append(("K", b, n))
                if b > 0:
                    items.append(("V", b - 1, n))
            if b > 0:
                items.append(("FIN", b - 1, 0))
            items.append(("GATE", b, 0))
        for n in range(32):
            items.append(("V", 3, n))
        items.append(("FIN", 3, 0))
        gather_list = [it for it in items if it[0] in ("K", "V")]
        gpos = [0]
        gslot = {"K": 0, "V": 0}

        def issue_gathers(upto):
            while gpos[0] < min(upto, len(gather_list)):
                kind, b, n = gather_list[gpos[0]]
                gather(kind.lower(), b, n, gslot[kind] % 3)
                gslot[kind] += 1
                gpos[0] += 1

        ipos = [0]
        gdone = [0]

        def run_items(k):
            for _ in range(k):
                if ipos[0] >= len(items):
                    return
                kind, b, n = items[ipos[0]]; ipos[0] += 1
                if kind == "K":
                    item_K(b, n)
                if kind == "V":
                    item_V(b, n)
                if kind in ("K", "V"):
                    gdone[0] += 1
                    issue_gathers(gdone[0] + PFD)
                if kind == "GATE":
                    flushK()
                    item_GATE(b)
                if kind == "FIN":
                    item_FIN(b)
        S.dma("pool", lambda e: e.dma_start(out=woB[:], in_=w_out[512:1024, :].rearrange("(h d) f -> d h f", d=64)), "woB", writes=["woB"])

        ctr = [0]

        pend = []

        def pump(flush=False):
            keep = []
            for ent in pend:
                ent[0] -= 1
                if ent[0] <= 0 or flush:
                    ent[1]()
                else:
                    keep.append(ent)
            pend[:] = keep

        def attn(ob, h):
            q0 = ob * 256
            kq = "QT.%d.%d" % (h, ob)
            PO = psA[(ob * 8 + h) % 2]
            kpo = "psA%d" % ((ob * 8 + h) % 2)
            fi = (ob * 8 + h) % 2
            OS, RC = o_sb[fi], rc[fi]
            units = [("own", ob)] + [("past", n) for n in range(ob)]

            def mk(ui, kind, n):
                i = ctr[0]; ctr[0] += 1
                PS = psZ[i % 2]; kps = "psZ%d" % (i % 2)
                PT = PTb[i % 3]; kpt = "PT%d" % (i % 3)
                last = (ui == len(units) - 1)
                if kind == "own":
                    c0 = 2 * ob * 128

                    def A():
                        S.op("pe", lambda e: e.matmul(PS[:, 0:128], lhsT=KT[0:72, h, c0:c0 + 128], rhs=QT[0:72, h, q0:q0 + 128],
                                                      start=True, stop=False), reads=[kq], writes=[kps])
                        S.op("pe", lambda e: e.matmul(PS[:, 0:128], lhsT=idn[:], rhs=trib[:], start=False, stop=True),
                             reads=["idn", "trib"], writes=[kps])
                        S.op("pe", lambda e: e.matmul(PS[:, 128:256], lhsT=KT[0:72, h, c0:c0 + 128], rhs=QT[0:72, h, q0 + 128:q0 + 256],
                                                      start=True, stop=True), reads=[kq], writes=[kps])
                        S.op("pe", lambda e: e.matmul(PS[:, 256:384], lhsT=KT[0:72, h, c0 + 128:c0 + 256],
                                                      rhs=QT[0:72, h, q0 + 128:q0 + 256], start=True, stop=False), reads=[kq], writes=[kps])
                        S.op("pe", lambda e: e.matmul(PS[:, 256:384], lhsT=idn[:], rhs=trib[:], start=False, stop=True),
                             reads=["idn", "trib"], writes=[kps])
                        S.op("act", lambda e: e.activation(out=PT[:, 0:384], in_=PS[:, 0:384], func=AF.Exp), reads=[kps], writes=[kpt])

                    def B():
                        S.op("pe", lambda e: e.matmul(PO[0:65, 0:256], lhsT=Vaug[:, 2 * ob, h, :], rhs=PT[:, 0:256],
                                                      start=True, stop=False), reads=[kpt], writes=[kpo])
                        S.op("pe", lambda e: e.matmul(PO[0:65, 128:256], lhsT=Vaug[:, 2 * ob + 1, h, :], rhs=PT[:, 256:384],
                                                      start=False, stop=last), reads=[kpt], writes=[kpo])
                else:
                    c0 = 2 * n * 128

                    def A():
                        S.op("pe", lambda e: e.matmul(PS[:, 0:256], lhsT=KT[0:72, h, c0:c0 + 128], rhs=QT[0:72, h, q0:q0 + 256],
                                                      start=True, stop=True), reads=[kq], writes=[kps])
                        S.op("pe", lambda e: e.matmul(PS[:, 256:512], lhsT=KT[0:72, h, c0 + 128:c0 + 256], rhs=QT[0:72, h, q0:q0 + 256],
                                                      start=True, stop=True), reads=[kq], writes=[kps])
                        S.op("act", lambda e: e.activation(out=PT[:, :], in_=PS[:, :], func=AF.Exp), reads=[kps], writes=[kpt])

                    def B():
                        S.op("pe", lambda e: e.matmul(PO[0:65, 0:256], lhsT=Vaug[:, 2 * n, h, :], rhs=PT[:, 0:256],
                                                      start=False, stop=False), reads=[kpt], writes=[kpo])
                        S.op("pe", lambda e: e.matmul(PO[0:65, 0:256], lhsT=Vaug[:, 2 * n + 1, h, :], rhs=PT[:, 256:512],
                                                      start=False, stop=last), reads=[kpt], writes=[kpo])
                return A, B, last

            def F1():
                S.op("act", lambda e: e.activation(out=OS[0:65, :], in_=PO[0:65, 0:256], func=AF.Copy), reads=[kpo], writes=["osb%d" % fi])
                S.op("dve", lambda e: e.reciprocal(out=RC[64:65, :], in_=OS[64:65, :]), reads=["osb%d" % fi], writes=["rc%d" % fi])

            def F2():
                S.op("pe", lambda e: e.matmul(psZ[2][0:64, 0:256], lhsT=onesf[64:65, 0:64], rhs=RC[64:65, :], start=True, stop=True),
                     reads=["rc%d" % fi, "onesf"], writes=["psZ2"])
                S.op("dve", lambda e: e.tensor_tensor(out=QT[0:64, h, q0:q0 + 256], in0=OS[0:64, :], in1=psZ[2][0:64, 0:256], op=ALU.mult),
                     reads=["osb%d" % fi, "psZ2"], writes=[kq])

            for ui, (kind, n) in enumerate(units):
                A, B, last = mk(ui, kind, n)
                A()
                if SAMPLE and ITEMS_PER_UNIT:
                    run_items(ITEMS_PER_UNIT)
                pump()
                if last:
                    pend.append([1, lambda B=B: (B(), F1())])
                    pend.append([2, F2])
                else:
                    pend.append([1, B])

        issue_gathers(PFD)
        for ob_ in range(8):
            for h_ in range(8):
                if not (FLAGS & 1024):
                    attn(ob_, h_)
                if SAMPLE and not ITEMS_PER_UNIT:
                    run_items(5)
        pump(flush=True)
        if SAMPLE:
            run_items(len(items))
        else:
            S.op("pool", lambda e: e.memset(oTs[:], 0.0), writes=["oTs"])
        S.barrier()

        if PHASES <= 2:
            return finish()
        def p3_tile(j):
            R = 128 if j < NT else NS
            r0 = 128 * j
            s = j % 2
            XT = xt3[s]; kx = "xt3_%d" % s
            ST = stt[s]; ks = "st3_%d" % s
            cur_par[0] = s
            S.dma("sp", lambda e, XT=XT, R=R, r0=r0: e.dma_start(out=XT[0:R, :], in_=x[r0:r0 + R, :]), kx, writes=[kx])
            for half in range(2):
                zb = (2 * j + half) % 4
                Z = psZ[zb]; kz = "psZ%d" % zb
                for g in range(4):
                    S.op("pe", lambda e, g=g, Z=Z, R=R, r0=r0, half=half: e.matmul(Z[0:R, :], lhsT=A_T[:, g, r0:r0 + R],
                                                                                    rhs=woA[:, g, half * 512:(half + 1) * 512], start=(g == 0), stop=False),
                         reads=["woA"], writes=[kz])
                for h in range(8):
                    lhs = QT[0:64, h, r0:r0 + R] if j < NT else oTs[0:64, h, 0:R]
                    S.op("pe", lambda e, h=h, Z=Z, R=R, lhs=lhs, half=half: e.matmul(Z[0:R, :], lhsT=lhs, rhs=woB[0:64, h, half * 512:(half + 1) * 512],
                                                                                      start=False, stop=(h == 7)),
                         reads=["woB", "oTs"], writes=[kz])
                S.op("dve", lambda e, Z=Z, R=R, j=j, half=half, XT=XT: e.tensor_tensor(out=h1[0:R, j, half * 512:(half + 1) * 512], in0=Z[0:R, :],
                                                                                        in1=XT[0:R, half * 512:(half + 1) * 512], op=ALU.add),
                     reads=[kz, kx], writes=["h1.%d.%d" % (j, half)])
            kh = "h1.%d" % j
            S.op("act", lambda e, R=R, j=j, ST=ST: e.activation(out=junk3[0:R, :], in_=h1[0:R, j, :], func=AF.Square, accum_out=ST[0:R, 0:1]),
                 reads=["h1.%d.0" % j, "h1.%d.1" % j], writes=["junk3", ks + ".0"])
            S.op("act", lambda e, R=R, ST=ST: e.activation(out=ST[0:R, 1:2], in_=ST[0:R, 0:1], func=AF.Sqrt, scale=1.0 / D, bias=epsc[0:R, :]),
                 reads=[ks + ".0"], writes=[ks + ".1"])
            S.op("dve", lambda e, R=R, ST=ST: e.reciprocal(out=ST[0:R, 2:3], in_=ST[0:R, 1:2]), reads=[ks + ".1"], writes=[ks + ".2"])
            S.op("pool", lambda e, R=R, j=j, ST=ST, s=s: e.tensor_scalar(out=hn3[s][0:R, :], in0=h1[0:R, j, :], scalar1=ST[0:R, 2:3], scalar2=None, op0=ALU.mult),
                 reads=["h1.%d.0" % j, "h1.%d.1" % j, ks + ".2"], writes=["hn3_%d" % s])
            norm_T(hn3[s], "hn3_%d" % s, R, 8, hn2T[:, :, r0:r0 + R], "hn2T.%d" % j)

        pipelined(p3_tile, NT + 1, HEADF[1])
        cur_par[0] = None
        S.barrier()

        if PHASES <= 3:
            return finish()
        def load_quarter(qd):
            sl = qd % 2
            S.dma("pool", lambda e: e.dma_start(out=wu[sl][:], in_=w_up[:, qd * 1024:(qd + 1) * 1024].rearrange("(k p) f -> p k f", p=128)),
                  "wu%d" % sl, writes=["wu%d" % sl])
            S.dma("pool", lambda e: e.dma_start(out=wd[sl][:], in_=w_down[qd * 1024:(qd + 1) * 1024, :].rearrange("(k p) f -> p k f", p=128)),
                  "wd%d" % sl, writes=["wd%d" % sl])

        def p4_block(qd, m):
            sl = qd % 2
            N = 512 if m < 4 else NS
            c0 = 512 * m
            ab = (qd * 5 + m) % 2
            for fl in range(8):
                ui = uc[0]; uc[0] += 1
                Z = psZ[ui % 2]; kz = "psZ%d" % (ui % 2)
                RT = rt[ui % 3]; krt = "rt%d" % (ui % 3)
                for kc in range(8):
                    S.op("pe", lambda e, kc=kc, fl=fl, Z=Z, N=N, c0=c0: e.matmul(Z[:, 0:N], lhsT=wu[sl][:, kc, fl * 128:(fl + 1) * 128],
                                                                                  rhs=hn2T[:, kc, c0:c0 + N], start=(kc == 0), stop=(kc == 7)),
                         reads=["wu%d" % sl], writes=[kz])
                S.op("act", lambda e, Z=Z, RT=RT, N=N: e.activation(out=RT[:, 0:N], in_=Z[:, 0:N], func=AF.Relu), reads=[kz], writes=[krt])
                S.op("pool", lambda e, RT=RT, N=N, fl=fl, ab=ab: e.tensor_tensor(out=aT[ab][:, fl, 0:N], in0=RT[:, 0:N], in1=RT[:, 0:N], op=ALU.mult),
                     reads=[krt], writes=["aT%d.%d" % (ab, fl)])
            tiles = [4 * m + t for t in range(4)] if m < 4 else [NT]
            for ti, j in enumerate(tiles):
                R = 128 if j < NT else NS
                for half in range(2):
                    di_ = (qd * 5 + m) * 8 + ti * 2 + half
                    if di_ % 4 < 2:
                        Z = psZ[2 + di_ % 2]; kz = "psZ%d" % (2 + di_ % 2)
                    else:
                        Z = psA[di_ % 2]; kz = "psA%d" % (di_ % 2)
                    for fl in range(8):
                        S.op("pe", lambda e, fl=fl, Z=Z, R=R, ti=ti, half=half, ab=ab: e.matmul(
                            Z[0:R, :], lhsT=aT[ab][:, fl, ti * 128:ti * 128 + R], rhs=wd[sl][:, fl, half * 512:(half + 1) * 512],
                            start=(fl == 0), stop=(fl == 7)),
                            reads=["aT%d.%d" % (ab, fl), "wd%d" % sl], writes=[kz])
                    S.op("dve", lambda e, Z=Z, R=R, j=j, half=half: e.tensor_tensor(out=h1[0:R, j, half * 512:(half + 1) * 512],
                                                                                    in0=Z[0:R, :], in1=h1[0:R, j, half * 512:(half + 1) * 512], op=ALU.add),
                         reads=[kz], writes=["h1.%d.%d" % (j, half)])


        load_quarter(0)
        load_quarter(1)
        uc = [0]
        for qd in range(4):
            sl = qd % 2
            if qd == 3:
                S.dma("pool", lambda e: e.dma_start(out=wpg[:], in_=w_pg.rearrange("(k p) f -> p k f", p=128)), "wpg", writes=["wu0", "wd0", "wpg"])
                S.dma("pool", lambda e: e.dma_start(out=wpp[:], in_=w_pp.rearrange("(k p) f -> p k f", p=128)), "wpp", writes=["wu0", "wd0", "wpp"])
            for m_ in range(5):
                p4_block(qd, m_)
            if qd < 2:
                load_quarter(qd + 2)
        S.barrier()

        if PHASES <= 4:
            return finish()
        def p5_tile(j):
            R = 128 if j < NT else NS
            r0 = 128 * j
            s = j % 2
            ST = stt[s]; ks = "st5_%d" % s
            cur_par[0] = s
            S.dma("sp", lambda e, R=R, r0=r0, s=s: e.dma_start(out=pt5[s][0:R, :], in_=pin[r0:r0 + R, :]), "pt5_%d" % s, writes=["pt5_%d" % s])
            S.op("act", lambda e, R=R, j=j, ST=ST: e.activation(out=junk5[0:R, :], in_=h1[0:R, j, :], func=AF.Square, accum_out=ST[0:R, 0:1]),
                 reads=["h1.%d.0" % j, "h1.%d.1" % j], writes=["junk5", ks + ".0"])
            S.op("act", lambda e, R=R, ST=ST: e.activation(out=ST[0:R, 1:2], in_=ST[0:R, 0:1], func=AF.Sqrt, scale=1.0 / D, bias=epsc[0:R, :]),
                 reads=[ks + ".0"], writes=[ks + ".1"])
            S.op("dve", lambda e, R=R, ST=ST: e.reciprocal(out=ST[0:R, 2:3], in_=ST[0:R, 1:2]), reads=[ks + ".1"], writes=[ks + ".2"])
            S.op("pool", lambda e, R=R, j=j, ST=ST, s=s: e.tensor_scalar(out=hn5[s][0:R, :], in0=h1[0:R, j, :], scalar1=ST[0:R, 2:3], scalar2=None, op0=ALU.mult),
                 reads=["h1.%d.0" % j, "h1.%d.1" % j, ks + ".2"], writes=["hn5_%d" % s])
            norm_T(hn5[s], "hn5_%d" % s, R, 16, hnT5[s][:, :, 0:R], "hnT5_%d" % s)
            S.op("pool", lambda e, R=R, s=s: e.tensor_copy(out=pb5[s][0:R, :], in_=pt5[s][0:R, :]), reads=["pt5_%d" % s], writes=["pb5_%d" % s])
            tb = tbank()
            for kc in range(2):
                S.op("pe", lambda e, kc=kc, tb=tb, R=R, s=s: e.transpose(out=psT[tb][:, kc * 128:kc * 128 + R], in_=pb5[s][0:R, kc * 128:(kc + 1) * 128],
                                                                         identity=idn[0:R, 0:R]),
                     reads=["pb5_%d" % s, "idn"], writes=["psT%d" % tb])
            S.op("dve", lambda e, tb=tb, R=R, s=s: e.tensor_copy(out=pT5[s][:, :, 0:R], in_=psT[tb][:, 0:256].rearrange("p (k t) -> p k t", k=2)[:, :, 0:R]),
                 reads=["psT%d" % tb], writes=["pT5_%d" % s])
            for half in range(2):
                hs = slice(half * 512, (half + 1) * 512)
                ZG = psZ[2 * s + half]; kzg = "psZ%d" % (2 * s + half)
                ZE = psA[s]; kze = "psA%d" % s
                for kc in range(8):
                    S.op("pe", lambda e, kc=kc, ZG=ZG, R=R, s=s, hs=hs: e.matmul(ZG[0:R, :], lhsT=hnT5[s][:, kc, 0:R], rhs=wpg[:, kc, hs],
                                                                                  start=(kc == 0), stop=(kc == 7)),
                         reads=["hnT5_%d" % s, "wpg"], writes=[kzg])
                S.op("act", lambda e, ZG=ZG, R=R, s=s, hs=hs: e.activation(out=sig5[s][0:R, hs], in_=ZG[0:R, :], func=AF.Sigmoid),
                     reads=[kzg], writes=["sig5_%d.%d" % (s, half)])
                for kc in range(2):
                    S.op("pe", lambda e, kc=kc, ZE=ZE, R=R, s=s, hs=hs: e.matmul(ZE[0:R, :], lhsT=pT5[s][:, kc, 0:R], rhs=wpp[:, kc, hs],
                                                                                  start=(kc == 0), stop=(kc == 1)),
                         reads=["pT5_%d" % s, "wpp"], writes=[kze])
                S.op("dve", lambda e, ZE=ZE, R=R, s=s, hs=hs: e.tensor_copy(out=e5[s][0:R, hs], in_=ZE[0:R, :]),
                     reads=[kze], writes=["e5_%d.%d" % (s, half)])
            ke = ["e5_%d.0" % s, "e5_%d.1" % s]
            S.op("act", lambda e, R=R, s=s, ST=ST: e.activation(out=junk5[0:R, :], in_=e5[s][0:R, :], func=AF.Square, accum_out=ST[0:R, 4:5]),
                 reads=ke, writes=["junk5", ks + ".4"])
            S.op("act", lambda e, R=R, ST=ST: e.activation(out=ST[0:R, 5:6], in_=ST[0:R, 4:5], func=AF.Sqrt, scale=1.0 / D, bias=epsc[0:R, :]),
                 reads=[ks + ".4"], writes=[ks + ".5"])
            S.op("dve", lambda e, R=R, ST=ST: e.reciprocal(out=ST[0:R, 6:7], in_=ST[0:R, 5:6]), reads=[ks + ".5"], writes=[ks + ".6"])
            S.op("dve", lambda e, R=R, s=s: e.tensor_tensor(out=t5[s][0:R, :], in0=e5[s][0:R, :], in1=GP[0:R, :], op=ALU.mult),
                 reads=ke + ["gp"], writes=["t5_%d" % s])
            S.op("dve", lambda e, R=R, s=s, ST=ST: e.scalar_tensor_tensor(out=t5[s][0:R, :], in0=t5[s][0:R, :], scalar=ST[0:R, 6:7], in1=sig5[s][0:R, :],
                                                                          op0=ALU.mult, op1=ALU.mult),
                 reads=["t5_%d" % s, ks + ".6", "sig5_%d.0" % s, "sig5_%d.1" % s], writes=["t5_%d" % s])
            S.op("dve", lambda e, R=R, s=s, j=j: e.tensor_tensor(out=y5[s][0:R, :], in0=t5[s][0:R, :], in1=h1[0:R, j, :], op=ALU.add),
                 reads=["t5_%d" % s, "h1.%d.0" % j, "h1.%d.1" % j], writes=["y5_%d" % s])
            S.dma("sp", lambda e, R=R, r0=r0, s=s: e.dma_start(out=y[r0:r0 + R, :], in_=y5[s][0:R, :]), "o_y%d" % s, reads=["y5_%d" % s], final=True)

        pipelined(p5_tile, NT + 1, HEADF[2])
        cur_par[0] = None

        return finish()


_NC_CACHE = {}


def _consts():
    c = np.zeros((128, 896), np.float32)
    c[:, 0:128] = np.eye(128, dtype=np.float32)
    c[:, 128:256] = np.tril(np.ones((128, 128), np.float32))
    kq = np.arange(128)
    c[:, 256:384] = np.where(kq[:, None] > kq[None, :], NEG, 0.0)
    oh = np.zeros((8, 8, 8), np.float32)
    for n in range(8):
        oh[n, :, n] = 1.0
    c[:, 384:896] = oh.reshape(1, 512)
    return c


def _bmask():
    m = np.zeros((32, 512), np.float32)
    for c in range(32):
        h = c // 4
        m[c, h * 64:(h + 1) * 64] = 1.0
    return m


def extra_inputs(inp, c):
    pt = np.asarray(inp["page_table"], dtype=np.int32)[4 * c:4 * c + 4]
    ck = np.ascontiguousarray(np.asarray(inp["cache_k"], dtype=np.float32)).reshape(NPOOL * 64, 1024)
    cv = np.ascontiguousarray(np.asarray(inp["cache_v"], dtype=np.float32)).reshape(NPOOL * 64, 1024)
    return dict(cache_k=ck, cache_v=cv,
                pt_even=np.ascontiguousarray(pt[:, 0::2]).reshape(128), pt_odd=np.ascontiguousarray(pt[:, 1::2]).reshape(128),
                iota64=(np.arange(128) % 64).astype(np.float32).reshape(128, 1), bmask=_bmask())


def kernel(x_prompt, x_sample, p_prompt, p_sample, cache_k, cache_v, page_table,
           ln1, w_in, a_v_norm, a_ws, a_bs, q_norm, k_norm, w_out,
           ln2, w_up, w_down, ln3, w_ple_gate, w_ple_proj, ple_norm):
    f = lambda a: np.ascontiguousarray(np.asarray(a, dtype=np.float32))
    if "nc" not in _NC_CACHE:
        _NC_CACHE["nc"] = build_nc()
    nc = _NC_CACHE["nc"]
    xs = f(x_sample).reshape(NCORES, NS, D)
    ps = f(p_sample)[0].reshape(NCORES, NS, 256)
    xp = f(x_prompt)
    pp = f(p_prompt)[0]
    shared = dict(
        consts=_consts(), ln1=f(ln1)[0], ln2=f(ln2)[0], ln3=f(ln3)[0], w_in=f(w_in)[0],
        a_v_norm=f(a_v_norm)[0].reshape(512), a_ws=f(a_ws)[0], a_bs=f(a_bs)[0],
        q_norm=f(q_norm)[0], k_norm=f(k_norm)[0], w_out=f(w_out)[0], w_up=f(w_up)[0],
        w_down=f(w_down)[0], w_pg=f(w_ple_gate)[0], w_pp=f(w_ple_proj)[0], ple_norm=f(ple_norm)[0],
    )
    in_maps = []
    for c in range(NCORES):
        m = dict(shared)
        m["x"] = np.concatenate([xp[c], xs[c]], axis=0)
        m["p"] = np.concatenate([pp[c], ps[c]], axis=0)
        m.update(extra_inputs(dict(page_table=page_table, cache_k=cache_k, cache_v=cache_v), c))
        in_maps.append(m)
    res = run_bass_kernel_spmd(nc, in_maps, core_ids=list(range(NCORES)))
    R = res.results
    y = np.stack([r["y"] for r in R])
    kk = np.stack([r["k_new"] for r in R])
    vv = np.stack([r["v_new"] for r in R])
    ast = np.stack([r["a_state"] for r in R])
    y_prompt = np.ascontiguousarray(y[:, :SEQ])
    y_sample = np.ascontiguousarray(y[:, SEQ:].reshape(32, 4, D))
    k_prompt = kk[:, :SEQ].reshape(1, 8, SEQ, 8, 64)
    v_prompt = vv[:, :SEQ].reshape(1, 8, SEQ, 8, 64)
    k_sample = kk[:, SEQ:].reshape(1, 32, 4, 8, 64)
    v_sample = vv[:, SEQ:].reshape(1, 32, 4, 8, 64)
    a_s = ast.reshape(1, 32, 4, 4, 128)
    return (y_prompt, y_sample, np.ascontiguousarray(k_prompt), np.ascontiguousarray(v_prompt),
            np.ascontiguousarray(k_sample), np.ascontiguousarray(v_sample), np.ascontiguousarray(a_s))
```

```python
from contextlib import ExitStack

import numpy as np
import concourse.bass as bass
import concourse.mybir as mybir
from concourse.bass_utils import run_bass_kernel_spmd

F32 = mybir.dt.float32
BF16 = mybir.dt.bfloat16
I32 = mybir.dt.int32
ALU = mybir.AluOpType
AF = mybir.ActivationFunctionType
AX = mybir.AxisListType

NCORES = 8
SEQ = 2048
NT = 16
NS = 16
NTOK = SEQ + NS
D = 1024
EPS = 1e-6
NEG = -30000.0
PHASES = 5
FLAGS = 511
SAMPLE = True
DBG = {}
PFD = 3
SKIP_SELF = False
ITEMS_PER_UNIT = 1
HEADF = (0.45, 0.5, 0.5)
NPOOL = 2560


class Sched:
    ENGS = ("pe", "act", "dve", "pool", "sp")

    def __init__(self, nc, stack):
        self.nc = nc
        self.prog = {e: [] for e in self.ENGS}
        self.esem = {e: stack.enter_context(nc.semaphore("s_" + e)) for e in ("pe", "act", "dve", "pool")}
        self.cnt = {e: 0 for e in self.esem}
        self.seen = {e: {} for e in self.ENGS}
        self.last_w = {}
        self.readers = {}
        self.dsem = {}
        self.dcnt = {}
        self.stack = stack
        self.final = []

    def _deps(self, reads, writes):
        deps = []
        for r in reads:
            t = self.last_w.get(r)
            if t is not None:
                deps.append(t)
        for w in writes:
            t = self.last_w.get(w)
            if t is not None:
                deps.append(t)
            deps.extend(self.readers.get(w, ()))
        return deps

    def _emit_waits(self, eng, deps):
        need = {}
        for (sem, val, src) in deps:
            if src == eng and (eng == "pe" or SKIP_SELF):
                continue
            k = id(sem)
            if self.seen[eng].get(k, 0) >= val:
                continue
            if k not in need or need[k][1] < val:
                need[k] = (sem, val)
        for k, (sem, val) in need.items():
            self.seen[eng][k] = val
            self.prog[eng].append(lambda e, sem=sem, val=val: e.wait_ge(sem, val))

    def _commit(self, tok, reads, writes):
        for w in writes:
            self.last_w[w] = tok
            self.readers[w] = []
        for r in reads:
            if r in writes:
                continue
            self.readers.setdefault(r, []).append(tok)

    _rec = None

    def record(self):
        self._rec = []

    def stop(self):
        r, self._rec = self._rec, None
        return r

    def replay(self, lst):
        for ent in lst:
            if ent[0] == "op":
                self.op(*ent[1:])
            else:
                self.dma(*ent[1:])

    @staticmethod
    def merge(a, b):
        out = []
        na, nb = len(a), len(b)
        ia = ib = 0
        while ia < na or ib < nb:
            if ib >= nb or (ia < na and ia * max(nb, 1) <= ib * max(na, 1)):
                out.append(a[ia]); ia += 1
            else:
                out.append(b[ib]); ib += 1
        return out

    def op(self, eng, fn, reads=(), writes=()):
        if self._rec is not None:
            self._rec.append(("op", eng, fn, tuple(reads), tuple(writes)))
            return None
        deps = self._deps(reads, writes)
        self._emit_waits(eng, deps)
        self.cnt[eng] += 1
        sem = self.esem[eng]
        tok = (sem, self.cnt[eng], eng)
        self.prog[eng].append(lambda e, fn=fn, sem=sem: fn(e).then_inc(sem, 1))
        self._commit(tok, reads, writes)
        return tok

    def dma(self, eng, fn, semkey, reads=(), writes=(), final=False):
        if self._rec is not None:
            self._rec.append(("dma", eng, fn, semkey, tuple(reads), tuple(writes), final))
            return None
        deps = self._deps(reads, writes)
        self._emit_waits(eng, deps)
        if semkey not in self.dsem:
            self.dsem[semkey] = self.stack.enter_context(self.nc.semaphore("d%d" % len(self.dsem)))
            self.dcnt[semkey] = 0
        self.dcnt[semkey] += 16
        sem = self.dsem[semkey]
        tok = (sem, self.dcnt[semkey], "dma")
        self.prog[eng].append(lambda e, fn=fn, sem=sem: fn(e).then_inc(sem, 16))
        self._commit(tok, reads, writes)
        if final:
            self.final.append(tok)
        return tok

    def barrier(self):
        toks = [(self.esem[e], self.cnt[e], e) for e in self.esem if self.cnt[e] > 0]
        toks += [(self.dsem[k], self.dcnt[k], "dma") for k in self.dsem]
        for eng in self.ENGS:
            self._emit_waits(eng, [(t[0], t[1], "bar") for t in toks])

    def run(self, block):
        self._emit_waits("sp", self.final)
        progs = self.prog

        @block.tensor
        def _(e):
            for f in progs["pe"]:
                f(e)

        @block.scalar
        def _(e):
            for f in progs["act"]:
                f(e)

        @block.vector
        def _(e):
            for f in progs["dve"]:
                f(e)

        @block.gpsimd
        def _(e):
            for f in progs["pool"]:
                f(e)

        @block.sync
        def _(e):
            for f in progs["sp"]:
                f(e)


def build_nc():
    nc = bass.Bass("TRN2", target_bir_lowering=False)
    di = lambda n, s, dt=F32: nc.dram_tensor(n, s, dt, kind="ExternalInput").ap()
    do = lambda n, s: nc.dram_tensor(n, s, F32, kind="ExternalOutput").ap()
    x = di("x", [NTOK, D])
    pin = di("p", [NTOK, 256])
    consts = di("consts", [128, 896])
    ln1 = di("ln1", [D]); ln2 = di("ln2", [D]); ln3 = di("ln3", [D])
    w_in = di("w_in", [D, 2560])
    a_v_norm = di("a_v_norm", [512])
    a_ws = di("a_ws", [4, 128, 128])
    a_bs = di("a_bs", [4, 128])
    q_norm = di("q_norm", [64]); k_norm = di("k_norm", [64])
    w_out = di("w_out", [D, D])
    w_up = di("w_up", [D, 4096]); w_down = di("w_down", [4096, D])
    w_pg = di("w_pg", [D, D]); w_pp = di("w_pp", [256, D])
    ple_norm = di("ple_norm", [D])
    ck = di("cache_k", [NPOOL * 64, 1024])
    cv = di("cache_v", [NPOOL * 64, 1024])
    pte = di("pt_even", [128], I32)
    pto = di("pt_odd", [128], I32)
    iota = di("iota64", [128, 1], F32)
    bmask_d = di("bmask", [32, 512])
    y = do("y", [NTOK, D])
    k_new = do("k_new", [NTOK, 512])
    v_new = do("v_new", [NTOK, 512])
    a_state = do("a_state", [NS, 512])

    B0 = 16512

    def at(name, shape, dt, kib):
        return nc.alloc_sbuf_tensor_at(name, shape, dt, offset=B0 + int(round(kib * 1024)))

    with ExitStack() as st:
        S = Sched(nc, st)
        pst = lambda name, shape, dt: st.enter_context(nc.psum_tensor(name, shape, dt))
        st.enter_context(nc.allow_non_contiguous_dma(reason="small param layouts"))

        cst = at("cst", [128, 896], F32, 204.25)
        idn = at("idn", [128, 128], BF16, 1.5)
        trib = at("trib", [128, 128], BF16, 1.75)
        gains = at("gains", [128, 1664], F32, 2.0)
        GQ, GK, GA, GP = gains[:, 0:64], gains[:, 64:128], gains[:, 128:640], gains[:, 640:1664]
        g123 = at("g123", [128, 24], F32, 8.5)
        bsT = at("bsT", [128, 4], F32, 8.625)
        bsS = at("bsS", [16, 4], F32, 8.65625)
        epsc = at("epsc", [128, 1], F32, 8.6875)
        onec = at("onec", [128, 2], BF16, 8.71875)
        onesb = onec[:, 1:2]
        onesf = at("onesf", [128, 64], F32, 8.75)
        ones128 = at("ones128", [1, 128], F32, 12.5)
        wsT = at("wsT", [128, 4, 128], BF16, 9.0)
        BD = at("BD", [16, 4, 16], BF16, 10.0)
        stt = [at("stt%d" % i, [128, 64], F32, 10.125 + 0.25 * i) for i in range(2)]
        kmT = at("kmT", [64, 8, 8], BF16, 10.625)
        QTs = at("QTs", [128, 8, NS], BF16, 10.75)
        KTs = at("KTs", [128, 8, NS], BF16, 11.0)
        oTs = at("oTs", [64, 8, NS], BF16, 11.25)
        vsb = at("vsb", [16, 512], BF16, 11.5)
        QT = at("QT", [128, 8, SEQ], BF16, 13.0)
        A_T = at("A_T", [128, 4, NTOK], BF16, 45.0)
        KT = at("KT", [128, 8, SEQ], BF16, 61.25)
        Vaug = at("Vaug", [128, NT, 8, 65], BF16, 93.25)
        win = at("win", [128, 8, 2560], BF16, 109.5)
        o = 149.5
        xt = [at("xt%d" % i, [128, D], F32, o + 4 * i) for i in range(2)]; o += 8
        junk = at("junk", [128, D], BF16, o); o += 2
        hn = [at("hn%d" % i, [128, D], BF16, o + 2 * i) for i in range(2)]; o += 4
        hnT = [at("hnT%d" % i, [128, 8, 128], BF16, o + 2 * i) for i in range(2)]; o += 4
        u_sb = [at("u%d" % i, [128, 512], BF16, o + i) for i in range(2)]; o += 2
        raw = [at("raw%d" % i, [128, 512], F32, o + 2 * i) for i in range(2)]; o += 4
        tmpa = [at("tmpa%d" % i, [128, 512], F32, o + 2 * i) for i in range(2)]; o += 4
        vanf = [at("vanf%d" % i, [128, 512], F32, o + 2 * i) for i in range(2)]; o += 4
        vab = [at("vab%d" % i, [128, 512], BF16, o + i) for i in range(2)]; o += 2
        kn = [at("kn%d" % i, [128, 512], F32, o + 2 * i) for i in range(2)]; o += 4
        vf = [at("vf%d" % i, [128, 512], F32, o + 2 * i) for i in range(2)]; o += 4
        Qst = [at("Qst%d" % i, [128, 8, 72], BF16, o + 1.125 * i) for i in range(2)]; o += 2.25
        Kst = [at("Kst%d" % i, [128, 8, 72], BF16, o + 1.125 * i) for i in range(2)]; o += 2.25
        Atok = [at("Atok%d" % i, [128, 512], BF16, o + i) for i in range(2)]; o += 2
        wsn = at("wsn", [128, 4, 128], F32, o); o += 2
        wsm = at("wsm", [128, 4, 128], BF16, o); o += 1
        kmh = at("kmh", [64, 8, 16], F32, o); o += 0.5
        gsb2 = [at("gsb%d" % i, [128, 8, 8], F32, o + 0.25 * i) for i in range(2)]; o += 0.5
        top82 = [at("top8%d" % i, [128, 8, 8], F32, o + 0.25 * i) for i in range(2)]; o += 0.5
        gtmp2 = [at("gtmp%d" % i, [128, 8, 8], F32, o + 0.25 * i) for i in range(2)]; o += 0.5
        oh = at("oh", [128, 8, 8, 8], BF16, o); o += 1
        assert o <= 204.25, o
        PTb = [at("PT%d" % i, [128, 512], BF16, 109.5 + i) for i in range(3)]
        o_sb = [at("osb%d" % i, [128, 256], F32, 112.5 + i) for i in range(2)]
        rc = [at("rc%d" % i, [128, 256], F32, 114.5 + i) for i in range(2)]
        o = 116.5
        ridx = at("ridx", [128, 128], I32, o); o += 0.5
        iot = at("iot", [128, 1], F32, o); o += 0.03125
        ridxf = at("ridxf", [128, 128], F32, o); o += 0.5
        BDQ = at("BDQ", [128, 4, 4, 8], BF16, o); o += 0.25
        Sown = at("Sown", [4, 32], F32, o); o += 0.125
        Pown = at("Pown", [4, 32], BF16, o); o += 0.09375
        kblk = [at("kblk%d" % i, [128, 2, 512], BF16, o + 2 * i) for i in range(3)]; o += 6
        vblk = [at("vblk%d" % i, [128, 2, 512], BF16, o + 2 * i) for i in range(3)]; o += 6
        ktp = [at("ktp%d" % i, [128, 4, 128], BF16, o + i) for i in range(4)]; o += 4
        S_all = at("S_all", [128, 64, 32], F32, o); o += 8
        PT_all = [at("PT_all%d" % i, [128, 64, 32], BF16, o + 4 * i) for i in range(2)]; o += 8
        gsum = at("gsum", [1, 2048], F32, o); o += 8
        gate_s = at("gate_s", [1, 1024], F32, o); o += 4
        top8s = at("top8s", [1, 32, 8], F32, o); o += 1
        biasr = at("biasr", [1, 1024], F32, o); o += 4
        vown = at("vown", [4, 4, 512], BF16, o); o += 4
        oc = at("oc", [32, 512], F32, o); o += 2
        bmask = at("bmask", [32, 512], F32, o); o += 2
        o64 = at("o64", [32, 64], F32, o); o += 0.25
        o64b = at("o64b", [32, 64], BF16, o); o += 0.125
        dens = at("dens", [32, 2], F32, o); o += 0.125
        assert o <= 183.75, o
        woA = at("woA", [128, 4, D], BF16, 183.75)
        woB = at("woB", [64, 8, D], BF16, 191.75)
        h1 = at("h1", [128, NT + 1, D], F32, 61.25)
        hn2T = at("hn2T", [128, 8, NTOK], BF16, 129.25)
        xt3 = [at("xt3_%d" % i, [128, D], F32, 161.5 + 4 * i) for i in range(2)]
        junk3 = at("junk3", [128, D], BF16, 169.5)
        hn3 = [at("hn3_%d" % i, [128, D], BF16, 171.5 + 2 * i) for i in range(2)]
        wu = [at("wu0", [128, 8, 1024], BF16, 13.0), at("wu1", [128, 8, 1024], BF16, 161.5)]
        wd = [at("wd0", [128, 8, 1024], BF16, 29.0), at("wd1", [128, 8, 1024], BF16, 177.5)]
        aT = [at("aT%d" % i, [128, 8, 512], BF16, 45.0 + 8 * i) for i in range(2)]
        rt = [at("rt%d" % i, [128, 512], BF16, 193.5 + i) for i in range(3)]
        wpg = at("wpg", [128, 8, D], BF16, 13.0)
        wpp = at("wpp", [128, 2, D], BF16, 29.0)
        o = 129.25
        pt5 = [at("pt5_%d" % i, [128, 256], F32, o + i) for i in range(2)]; o += 2
        pb5 = [at("pb5_%d" % i, [128, 256], BF16, o + 0.5 * i) for i in range(2)]; o += 1
        pT5 = [at("pT5_%d" % i, [128, 2, 128], BF16, o + 0.5 * i) for i in range(2)]; o += 1
        hn5 = [at("hn5_%d" % i, [128, D], BF16, o + 2 * i) for i in range(2)]; o += 4
        hnT5 = [at("hnT5_%d" % i, [128, 8, 128], BF16, o + 2 * i) for i in range(2)]; o += 4
        sig5 = [at("sig5_%d" % i, [128, D], F32, o + 4 * i) for i in range(2)]; o += 8
        e5 = [at("e5_%d" % i, [128, D], F32, o + 4 * i) for i in range(2)]; o += 8
        t5 = [at("t5_%d" % i, [128, D], F32, o + 4 * i) for i in range(2)]; o += 8
        y5 = [at("y5_%d" % i, [128, D], F32, o + 4 * i) for i in range(2)]; o += 8
        junk5 = at("junk5", [128, D], BF16, o); o += 2
        assert o <= 183.75, o

        DBG.update(dict(QTs=QTs, KTs=KTs, PT1=PT_all[1], oTs=oTs, gate_s=gate_s, top8s=top8s, biasr=biasr, dens=dens, o64=o64, vown=vown, Pown=Pown, ridx=ridx, kblk0=kblk[0], S_all=S_all, BDQ=BDQ, vblk0=vblk[0], vblk1=vblk[1], vblk2=vblk[2], oc=oc, bmask=bmask))

        def finish():
            block = st.enter_context(nc.Block())
            S.run(block)
            return nc

        psT = [pst("psT%d" % i, [128, 1024], BF16) for i in range(2)]
        psZ = [pst("psZ%d" % i, [128, 512], F32) for i in range(4)]
        psA = [pst("psA%d" % i, [128, 512], F32) for i in range(2)]
        tcount = [0]

        cur_par = [None]

        def tbank():
            if cur_par[0] is not None:
                return cur_par[0]
            tcount[0] += 1
            return tcount[0] % 2

        ZB = [[(psZ[0], "psZ0"), (psZ[1], "psZ1"), (psA[0], "psA0")], [(psZ[2], "psZ2"), (psZ[3], "psZ3"), (psA[1], "psA1")]]

        S.dma("sp", lambda e: e.dma_start(out=cst[:], in_=consts[:, :]), "cst", writes=["cst"])
        S.dma("sp", lambda e: e.dma_start(out=gains[:, 0:64], in_=q_norm.partition_broadcast(128)), "gq", writes=["gq"])
        S.dma("sp", lambda e: e.dma_start(out=gains[:, 64:128], in_=k_norm.partition_broadcast(128)), "gk", writes=["gk"])
        S.dma("sp", lambda e: e.dma_start(out=gains[:, 128:640], in_=a_v_norm.partition_broadcast(128)), "ga", writes=["ga"])
        S.dma("sp", lambda e: e.dma_start(out=gains[:, 640:1664], in_=ple_norm.partition_broadcast(128)), "gp", writes=["gp"])
        for i, l in enumerate((ln1, ln2, ln3)):
            S.dma("sp", lambda e, i=i, l=l: e.dma_start(out=g123[:, 8 * i:8 * i + 8], in_=l.rearrange("(k p) -> p k", p=128)),
                  "g%d" % i, writes=["g123.%d" % i])
        S.dma("sp", lambda e: e.dma_start(out=bsT[:], in_=a_bs.rearrange("g t -> t g")), "bsT", writes=["bsT"])
        for b in range(4):
            S.dma("sp", lambda e, b=b: e.dma_start(out=bsS[4 * b:4 * b + 4, :], in_=a_bs[:, 0:4].rearrange("g t -> t g")),
                  "bsS", writes=["bsS"])
        S.dma("sp", lambda e: e.dma_start(out=wsn[:], in_=a_ws.rearrange("g t s -> t g s")), "wsn", writes=["wsn"])
        for kc in range(8):
            S.dma("pool", lambda e, kc=kc: e.dma_start(out=win[:, kc, :], in_=w_in[kc * 128:(kc + 1) * 128, :]),
                  "win%d" % kc, writes=["win%d" % kc])
        S.op("dve", lambda e: e.tensor_copy(out=idn[:], in_=cst[:, 0:128]), reads=["cst"], writes=["idn"])
        S.op("dve", lambda e: e.tensor_copy(out=trib[:], in_=cst[:, 256:384]), reads=["cst"], writes=["trib"])
        S.op("dve", lambda e: e.memset(epsc[:], EPS), writes=["epsc"])
        S.op("dve", lambda e: e.memset(onec[:, 0:1], 1.0 / 256), writes=["onec"])
        S.op("dve", lambda e: e.memset(onesf[:], 1.0), writes=["onesf"])
        S.op("dve", lambda e: e.memset(ones128[:], 1.0), writes=["ones128"])
        S.op("dve", lambda e: e.memset(onec[:, 1:2], 1.0), writes=["onesb"])
        for i in range(2):
            S.op("dve", lambda e, i=i: e.memset(gsb2[i][:], -3.0e38), writes=["gsb%d" % i])
        S.op("dve", lambda e: e.tensor_copy(out=oh[:].rearrange("p a b c -> p (a b c)"), in_=cst[:, 384:896]), reads=["cst"], writes=["oh"])
        for i in range(2):
            S.op("dve", lambda e, i=i: e.memset(Kst[i][:], 0.0), writes=["Kst%d.a" % i, "Kst%d.b" % i])
        if FLAGS & 32:
            S.op("pool", lambda e: e.memset(Vaug[:], 1.0), writes=["Vaug"])
        for i in range(2):
            S.op("pool", lambda e, i=i: e.memset(Qst[i][:, :, 64:72], 0.0), writes=["Qst%d.b" % i])
        S.op("dve", lambda e: e.tensor_scalar(out=gains[:, 0:64], in0=gains[:, 0:64], scalar1=0.125, scalar2=None, op0=ALU.mult),
             reads=["gq"], writes=["gq"])
        S.op("dve", lambda e: e.tensor_tensor(out=wsm[:], in0=wsn[:], in1=cst[:, 128:256].unsqueeze(1).to_broadcast([128, 4, 128]), op=ALU.mult),
             reads=["wsn", "cst"], writes=["wsm"])
        tb = tbank()
        for g in range(4):
            S.op("pe", lambda e, g=g, tb=tb: e.transpose(out=psT[tb][:, g * 128:(g + 1) * 128], in_=wsm[:, g, :], identity=idn[:]),
                 reads=["wsm", "idn"], writes=["psT%d" % tb])
        S.op("act", lambda e, tb=tb: e.activation(out=wsT[:], in_=psT[tb][:, 0:512].rearrange("p (g t) -> p g t", g=4), func=AF.Copy),
             reads=["psT%d" % tb], writes=["wsT"])
        S.op("dve", lambda e: e.memset(BD[:], 0.0), writes=["BD"])
        for b in range(4):
            S.dma("sp", lambda e, b=b: e.dma_start(out=BD[4 * b:4 * b + 4, :, 4 * b:4 * b + 4], in_=wsT[0:4, :, 0:4]),
                  "BD", reads=["wsT"], writes=["BD"])

        def rms_rstd(ST, R, ks, src, kin, junk_t, kjunk, n):
            S.op("act", lambda e: e.activation(out=junk_t[0:R, :], in_=src, func=AF.Square, accum_out=ST[0:R, 0:1]),
                 reads=[kin], writes=[kjunk, ks + ".0"])
            S.op("act", lambda e: e.activation(out=ST[0:R, 1:2], in_=ST[0:R, 0:1], func=AF.Ln, scale=1.0 / n, bias=epsc[0:R, :]),
                 reads=[ks + ".0", "epsc"], writes=[ks + ".1"])
            S.op("act", lambda e: e.activation(out=ST[0:R, 2:3], in_=ST[0:R, 1:2], func=AF.Exp, scale=-0.5), reads=[ks + ".1"], writes=[ks + ".2"])

        def norm_T(HN, khn, R, goff, dst, kdst):
            tb = tbank()
            for kc in range(8):
                S.op("pe", lambda e, kc=kc: e.transpose(out=psT[tb][:, kc * 128:kc * 128 + R], in_=HN[0:R, kc * 128:(kc + 1) * 128],
                                                        identity=idn[0:R, 0:R]),
                     reads=[khn, "idn"], writes=["psT%d" % tb])
            S.op("dve", lambda e: e.tensor_tensor(out=dst, in0=psT[tb][:, :].rearrange("p (k t) -> p k t", k=8)[:, :, 0:R],
                                                  in1=g123[:, goff:goff + 8].unsqueeze(2).to_broadcast([128, 8, R]), op=ALU.mult),
                 reads=["psT%d" % tb, "g123.%d" % (goff // 8)], writes=[kdst])

        h8 = lambda ap: ap.rearrange("p (h d) -> p h d", h=8)
        g4 = lambda ap: ap.rearrange("p (g w) -> p g w", g=4)

        def q_transposes(j, s, R, r0):
            tb = tbank()
            for h in range(8):
                S.op("pe", lambda e, h=h: e.transpose(out=psT[tb][0:72, h * 128:h * 128 + R], in_=Qst[s][0:R, h, 0:72], identity=idn[0:R, 0:R]),
                     reads=["Qst%d.a" % s, "Qst%d.b" % s, "idn"], writes=["psT%d" % tb])
            src = psT[tb][0:72, :].rearrange("p (h t) -> p h t", h=8)[:, :, 0:R]
            if j < NT:
                S.op("act", lambda e: e.activation(out=QT[0:72, :, r0:r0 + R], in_=src, func=AF.Copy),
                     reads=["psT%d" % tb], writes=["QT.%d" % j])
            else:
                S.op("act", lambda e: e.activation(out=QTs[0:72, :, 0:R], in_=src, func=AF.Copy), reads=["psT%d" % tb], writes=["QTs"])


        def pipelined(fn, n, frac):
            lists = []
            for j in range(n):
                S.record()
                fn(j)
                lists.append(S.stop())
            if frac <= 0:
                for l in lists:
                    S.replay(l)
                return
            cut = [int(len(l) * frac) for l in lists]
            S.replay(lists[0][:cut[0]])
            for j in range(n):
                tail = lists[j][cut[j]:]
                nxt = lists[j + 1][:cut[j + 1]] if j + 1 < n else []
                S.replay(Sched.merge(tail, nxt))

        def p1_tile(j):
            R = 128 if j < NT else NS
            r0 = 128 * j
            s = j % 2
            XT, ST, HN, HNT = xt[s], stt[s], hn[s], hnT[s]
            kx, ks = "xt%d" % s, "st%d" % s
            cur_par[0] = s
            gsb, top8, gtmp = gsb2[s], top82[s], gtmp2[s]
            kg = "gsb%d" % s
            S.dma("sp", lambda e: e.dma_start(out=XT[0:R, :], in_=x[r0:r0 + R, :]), kx, writes=[kx])
            rms_rstd(ST, R, ks, XT[0:R, :], kx, junk, "junk", D)
            S.op("act", lambda e: e.activation(out=HN[0:R, :], in_=XT[0:R, :], func=AF.Copy, scale=ST[0:R, 2:3]),
                 reads=[kx, ks + ".2"], writes=["hn%d" % s])
            norm_T(HN, "hn%d" % s, R, 0, HNT[:, :, 0:R], "hnT%d" % s)
            zc = [0]

            def zbank():
                zc[0] += 1
                return ZB[s][zc[0] % 3]

            for c in range(5):
                Z, kz = zbank()
                for kc in range(8):
                    S.op("pe", lambda e, kc=kc, c=c, Z=Z: e.matmul(Z[0:R, :], lhsT=HNT[:, kc, 0:R],
                                                                   rhs=win[:, kc, c * 512:(c + 1) * 512], start=(kc == 0), stop=(kc == 7)),
                         reads=["hnT%d" % s, "win%d" % kc], writes=[kz])
                if c == 0:
                    S.op("act", lambda e, Z=Z: e.activation(out=u_sb[s][0:R, :], in_=Z[0:R, :], func=AF.Gelu_apprx_tanh),
                         reads=[kz], writes=["u%d" % s])
                elif c == 1:
                    S.op("act", lambda e, Z=Z: e.activation(out=raw[s][0:R, :], in_=Z[0:R, :], func=AF.Gelu_apprx_tanh),
                         reads=[kz], writes=["raw%d" % s])
                    S.op("dve", lambda e: e.tensor_tensor(out=tmpa[s][0:R, :], in0=raw[s][0:R, :], in1=raw[s][0:R, :], op=ALU.mult),
                         reads=["raw%d" % s], writes=["tmpa%d" % s])
                    S.op("dve", lambda e: e.tensor_reduce(out=ST[0:R, 4:8], in_=g4(tmpa[s][0:R, :]), axis=AX.X, op=ALU.add),
                         reads=["tmpa%d" % s], writes=[ks + ".a"])
                    S.op("act", lambda e: e.activation(out=ST[0:R, 8:12], in_=ST[0:R, 4:8], func=AF.Ln, scale=1.0 / 128, bias=epsc[0:R, :]),
                         reads=[ks + ".a", "epsc"], writes=[ks + ".b"])
                    S.op("act", lambda e: e.activation(out=ST[0:R, 12:16], in_=ST[0:R, 8:12], func=AF.Exp, scale=-0.5), reads=[ks + ".b"], writes=[ks + ".c"])
                    S.op("dve", lambda e: e.tensor_tensor(out=g4(tmpa[s][0:R, :]), in0=g4(raw[s][0:R, :]),
                                                          in1=ST[0:R, 12:16].unsqueeze(2).to_broadcast([R, 4, 128]), op=ALU.mult),
                         reads=["raw%d" % s, ks + ".c"], writes=["tmpa%d" % s])
                    S.op("dve", lambda e: e.tensor_tensor(out=vanf[s][0:R, :], in0=tmpa[s][0:R, :], in1=GA[0:R, :], op=ALU.mult),
                         reads=["tmpa%d" % s, "ga"], writes=["vanf%d" % s])
                    S.op("act", lambda e: e.activation(out=vab[s][0:R, :], in_=vanf[s][0:R, :], func=AF.Copy),
                         reads=["vanf%d" % s], writes=["vab%d" % s])
                    if j == NT:
                        S.dma("sp", lambda e: e.dma_start(out=a_state[:, :], in_=vanf[s][0:R, :]), "o_vanf%d" % s,
                              reads=["vanf%d" % s], final=True)
                elif c in (2, 3):
                    o0 = 16 if c == 2 else 40
                    S.op("act", lambda e, Z=Z: e.activation(out=raw[s][0:R, :], in_=Z[0:R, :], func=AF.Copy), reads=[kz], writes=["raw%d" % s])
                    S.op("act", lambda e, Z=Z: e.activation(out=tmpa[s][0:R, :], in_=Z[0:R, :], func=AF.Square), reads=[kz], writes=["tmpa%d" % s])
                    S.op("dve", lambda e, o0=o0: e.tensor_reduce(out=ST[0:R, o0:o0 + 8], in_=h8(tmpa[s][0:R, :]), axis=AX.X, op=ALU.add),
                         reads=["tmpa%d" % s], writes=[ks + ".q%d" % c])
                    S.op("act", lambda e, o0=o0: e.activation(out=ST[0:R, o0 + 8:o0 + 16], in_=ST[0:R, o0:o0 + 8], func=AF.Ln, scale=1.0 / 64,
                                                              bias=epsc[0:R, :]),
                         reads=[ks + ".q%d" % c, "epsc"], writes=[ks + ".r%d" % c])
                    S.op("act", lambda e, o0=o0: e.activation(out=ST[0:R, o0 + 16:o0 + 24], in_=ST[0:R, o0 + 8:o0 + 16], func=AF.Exp, scale=-0.5),
                         reads=[ks + ".r%d" % c], writes=[ks + ".s%d" % c])
                    S.op("dve", lambda e, o0=o0: e.tensor_tensor(out=h8(tmpa[s][0:R, :]), in0=h8(raw[s][0:R, :]),
                                                                 in1=ST[0:R, o0 + 16:o0 + 24].unsqueeze(2).to_broadcast([R, 8, 64]), op=ALU.mult),
                         reads=["raw%d" % s, ks + ".s%d" % c], writes=["tmpa%d" % s])
                    if c == 2:
                        S.op("dve", lambda e: e.tensor_tensor(out=Qst[s][0:R, :, 0:64], in0=h8(tmpa[s][0:R, :]),
                                                              in1=GQ[0:R, :].unsqueeze(1).to_broadcast([R, 8, 64]), op=ALU.mult),
                             reads=["tmpa%d" % s, "gq"], writes=["Qst%d.a" % s])
                    else:
                        S.op("dve", lambda e: e.tensor_tensor(out=h8(kn[s][0:R, :]), in0=h8(tmpa[s][0:R, :]),
                                                              in1=GK[0:R, :].unsqueeze(1).to_broadcast([R, 8, 64]), op=ALU.mult),
                             reads=["tmpa%d" % s, "gk"], writes=["kn%d" % s])
                        S.dma("sp", lambda e: e.dma_start(out=k_new[r0:r0 + R, :], in_=kn[s][0:R, :]), "o_kn%d" % s,
                              reads=["kn%d" % s], final=True)
                        S.op("act", lambda e: e.activation(out=Kst[s][0:R, :, 0:64], in_=h8(kn[s][0:R, :]), func=AF.Copy),
                             reads=["kn%d" % s], writes=["Kst%d.a" % s])
                else:
                    S.op("act", lambda e, Z=Z: e.activation(out=vf[s][0:R, :], in_=Z[0:R, :], func=AF.Copy), reads=[kz], writes=["vf%d" % s])
                    S.dma("sp", lambda e: e.dma_start(out=v_new[r0:r0 + R, :], in_=vf[s][0:R, :]), "o_vf%d" % s,
                          reads=["vf%d" % s], final=True)
                    if j == NT:
                        S.op("act", lambda e: e.activation(out=vsb[0:R, :], in_=vf[s][0:R, :], func=AF.Copy), reads=["vf%d" % s], writes=["vsb"])
                    if j < NT and (FLAGS & 16):
                        S.op("pool", lambda e: e.tensor_copy(out=Vaug[:, j, :, 0:64], in_=h8(vf[s][:, :])),
                             reads=["vf%d" % s], writes=["Vaug"])
            if not (FLAGS & 1):
                return
            Z, kz = zbank()
            for g in range(4):
                lhs = wsT[:, g, :] if j < NT else BD[0:NS, g, :]
                S.op("pe", lambda e, g=g, lhs=lhs, Z=Z: e.matmul(Z[0:R, g * 128:(g + 1) * 128], lhsT=lhs, rhs=vab[s][0:R, g * 128:(g + 1) * 128],
                                                                 start=True, stop=True),
                     reads=["wsT", "BD", "vab%d" % s], writes=[kz])
            bs_t = bsT if j < NT else bsS
            for g in range(4):
                S.op("dve", lambda e, g=g, Z=Z: e.scalar_tensor_tensor(out=Atok[s][0:R, g * 128:(g + 1) * 128], in0=Z[0:R, g * 128:(g + 1) * 128],
                                                                       scalar=bs_t[0:R, g:g + 1], in1=u_sb[s][0:R, g * 128:(g + 1) * 128],
                                                                       op0=ALU.add, op1=ALU.mult),
                     reads=[kz, "bsT", "bsS", "u%d" % s], writes=["Atok%d" % s])
            tba = tbank()
            for g in range(4):
                S.op("pe", lambda e, g=g: e.transpose(out=psT[tba][:, g * 128:g * 128 + R], in_=Atok[s][0:R, g * 128:(g + 1) * 128],
                                                      identity=idn[0:R, 0:R]),
                     reads=["Atok%d" % s, "idn"], writes=["psT%d" % tba])
            S.op("act", lambda e: e.activation(out=A_T[:, :, r0:r0 + R], in_=psT[tba][:, 0:512].rearrange("p (g t) -> p g t", g=4)[:, :, 0:R],
                                               func=AF.Copy),
                 reads=["psT%d" % tba], writes=["A_T.%d" % j])
            if not (FLAGS & 2):
                return
            if j == NT and not (FLAGS & 256):
                return
            if j < NT and (FLAGS & 512):
                return
            ob = j // 2
            if j < NT and (FLAGS & 64):
                S.op("act", lambda e: e.activation(out=Kst[s][:, :, 64:72], in_=oh[:, ob, :, :], func=AF.Copy), reads=["oh"], writes=["Kst%d.b" % s])
            tb = tbank()
            KW = 72 if j < NT else 64
            for h in range(8):
                S.op("pe", lambda e, h=h: e.transpose(out=psT[tb][0:KW, h * 128:h * 128 + R], in_=Kst[s][0:R, h, 0:KW], identity=idn[0:R, 0:R]),
                     reads=["Kst%d.a" % s, "Kst%d.b" % s, "idn"], writes=["psT%d" % tb])
            src = psT[tb][0:KW, :].rearrange("p (h t) -> p h t", h=8)[:, :, 0:R]
            if j < NT:
                S.op("dve", lambda e: e.tensor_copy(out=KT[0:72, :, r0:r0 + R], in_=src), reads=["psT%d" % tb], writes=["KT.%d" % j])
                if not (FLAGS & 128):
                    return
                Z2, kz2 = zbank()
                for h in range(8):
                    S.op("pe", lambda e, h=h, Z2=Z2: e.matmul(Z2[0:64, h:h + 1], lhsT=Kst[s][:, h, 0:64], rhs=onec[:, 0:1], start=True, stop=True),
                         reads=["Kst%d.a" % s, "onec"], writes=[kz2])
                S.op("dve", lambda e, Z2=Z2: e.tensor_copy(out=kmh[0:64, :, j], in_=Z2[0:64, 0:8]), reads=[kz2], writes=["kmh.%d" % j])
                if j % 2 == 1:
                    S.op("dve", lambda e: e.tensor_tensor(out=kmT[0:64, :, ob], in0=kmh[0:64, :, j - 1], in1=kmh[0:64, :, j], op=ALU.add),
                         reads=["kmh.%d" % (j - 1), "kmh.%d" % j], writes=["kmT.%d" % ob])
            else:
                S.op("dve", lambda e: e.tensor_copy(out=KTs[0:64, :, 0:R], in_=src), reads=["psT%d" % tb], writes=["KTs"])
            if not (FLAGS & 4):
                return
            q_transposes(j, s, R, r0)
            if j < NT and ob >= 4 and (FLAGS & 8):
                Z3, kz3 = zbank()
                for h in range(8):
                    S.op("pe", lambda e, h=h, Z3=Z3: e.matmul(Z3[:, h * 8:h * 8 + ob], lhsT=QT[0:64, h, r0:r0 + 128], rhs=kmT[0:64, h, 0:ob],
                                                              start=True, stop=True),
                         reads=["QT.%d" % j] + ["kmT.%d" % n for n in range(ob)], writes=[kz3])
                S.op("dve", lambda e, Z3=Z3: e.tensor_copy(out=gsb[:, :, 0:ob], in_=Z3[:, 0:64].rearrange("p (h n) -> p h n", h=8)[:, :, 0:ob]),
                     reads=[kz3], writes=[kg])
                for h in range(8):
                    S.op("dve", lambda e, h=h: e.max(out=top8[:, h, :], in_=gsb[:, h, :]), reads=[kg], writes=[kg + ".t%d" % h])
                S.op("dve", lambda e: e.tensor_tensor(out=gtmp[:, :, 0:ob], in0=gsb[:, :, 0:ob], in1=top8[:, :, 2:3].to_broadcast([128, 8, ob]),
                                                      op=ALU.is_lt),
                     reads=[kg] + [kg + ".t%d" % h for h in range(8)], writes=[kg + ".g"])
                S.op("dve", lambda e: e.tensor_scalar(out=Qst[s][:, :, 64:64 + ob], in0=gtmp[:, :, 0:ob], scalar1=NEG, scalar2=None, op0=ALU.mult),
                     reads=[kg + ".g"], writes=["Qst%d.b" % s])
                q_transposes(j, s, R, r0)

        pipelined(p1_tile, NT + 1, HEADF[0])
        cur_par[0] = None
        if PHASES <= 1:
            return finish()
        S.barrier()
        S.dma("pool", lambda e: e.dma_start(out=woA[:], in_=w_out[0:512, :].rearrange("(g p) f -> p g f", p=128)), "woA", writes=["woA"])
        S.dma("sp", lambda e: e.dma_start(out=ridx[0:64, :], in_=pte.partition_broadcast(64)), "ridxa", writes=["ridx.a"])
        S.dma("sp", lambda e: e.dma_start(out=ridx[64:128, :], in_=pto.partition_broadcast(64)), "ridxb", writes=["ridx.b"])
        S.dma("sp", lambda e: e.dma_start(out=iot[:], in_=iota[:, :]), "iot", writes=["iot"])
        S.dma("sp", lambda e: e.dma_start(out=bmask[:], in_=bmask_d[:, :]), "bmask", writes=["bmask"])
        S.op("dve", lambda e: e.tensor_copy(out=ridxf[:], in_=ridx[:]), reads=["ridx.a", "ridx.b"], writes=["ridxf"])
        S.op("dve", lambda e: e.tensor_scalar(out=ridxf[:], in0=ridxf[:], scalar1=64.0, scalar2=iot[:, 0:1], op0=ALU.mult, op1=ALU.add),
             reads=["ridxf", "iot"], writes=["ridxf"])
        S.op("dve", lambda e: e.tensor_copy(out=ridx[:], in_=ridxf[:]), reads=["ridxf"], writes=["ridx"])
        S.op("dve", lambda e: e.memset(BDQ[:], 0.0), writes=["BDQ"])
        QTs4 = QTs[0:64, :, :].rearrange("p (hp two) t -> p hp two t", two=2)
        for b in range(4):
            S.op("dve", lambda e, b=b: e.tensor_copy(out=BDQ[0:64, b, :, 0:4], in_=QTs4[:, :, 0, 4 * b:4 * b + 4]), reads=["QTs", "BDQ"], writes=["BDQ"])
            S.dma("sp", lambda e, b=b: e.dma_start(out=BDQ[64:128, b, :, 4:8], in_=QTs4[:, :, 1, 4 * b:4 * b + 4]), "BDQ", reads=["QTs"], writes=["BDQ"])
            S.dma("sp", lambda e, b=b: e.dma_start(out=vown[0:4, b, :], in_=vsb[4 * b:4 * b + 4, :]), "vown", reads=["vsb"], writes=["vown"])

        gq_ = []

        def gather(kind, b, n, slot):
            dst = (kblk if kind == "k" else vblk)[slot]
            src = ck if kind == "k" else cv
            key = "%sblk%d" % (kind, slot)
            col = b * 32 + n
            S.dma("pool", lambda e: e.indirect_dma_start(out=dst[:].rearrange("p r f -> p (r f)"), out_offset=None, in_=src[:, :],
                                                         in_offset=bass.IndirectOffsetOnAxis(ap=ridx[:, col:col + 1], axis=0)),
                  key, reads=["ridx"], writes=[key])

        kcnt = [0]; vcnt = [0]; tpc = [0]

        pendK = []

        def flushK():
            while pendK:
                pendK.pop(0)()

        def item_K(b, n):
            i = kcnt[0]; kcnt[0] += 1
            slot = i % 3
            KB = kblk[slot]; kk = "kblk%d" % slot
            kps = []
            for r in range(2):
                t = tpc[0]; tpc[0] += 1
                tb = t % 2
                KP = ktp[t % 4]; kkp = "ktp%d" % (t % 4)
                kps.append((KP, kkp))
                for hp in range(4):
                    S.op("pe", lambda e, hp=hp, r=r, tb=tb: e.transpose(out=psT[tb][:, hp * 128:(hp + 1) * 128], in_=KB[:, r, hp * 128:(hp + 1) * 128],
                                                                     identity=idn[:]), reads=[kk, "idn"], writes=["psT%d" % tb])
                if t % 2 == 0:
                    S.op("act", lambda e, tb=tb, KP=KP: e.activation(out=KP[:].rearrange("p h t -> p (h t)"), in_=psT[tb][:, 0:512], func=AF.Copy),
                         reads=["psT%d" % tb], writes=[kkp])
                else:
                    S.op("dve", lambda e, tb=tb, KP=KP: e.tensor_copy(out=KP[:].rearrange("p h t -> p (h t)"), in_=psT[tb][:, 0:512]),
                         reads=["psT%d" % tb], writes=[kkp])

            def M():
                for r in range(2):
                    KP, kkp = kps[r]
                    for hp in range(4):
                        S.op("pe", lambda e, hp=hp, r=r, KP=KP: e.matmul(psZ[2][:, 256 + r * 32 + hp * 8:256 + r * 32 + hp * 8 + 8], lhsT=KP[:, hp, :], rhs=BDQ[:, b, hp, :],
                                                                         start=True, stop=True), reads=[kkp, "BDQ"], writes=["psZ2"])
                S.op("dve", lambda e: e.tensor_copy(out=S_all[:, 2 * n:2 * n + 2, :], in_=psZ[2][:, 256:320].rearrange("p (r c) -> p r c", r=2)),
                     reads=["psZ2"], writes=["S_all"])

            flushK()
            pendK.append(M)

        def item_GATE(b):
            pi = b % 2
            PTA = PT_all[pi]; kpta = "PT_all%d" % pi
            Sf = S_all[:].rearrange("p g c -> p (g c)")
            for q4 in range(4):
                S.op("pe", lambda e, q4=q4: e.matmul(psZ[2][0:1, 0:512], lhsT=onesf[:, 0:1], rhs=Sf[:, q4 * 512:(q4 + 1) * 512], start=True, stop=True),
                     reads=["S_all", "onesf"], writes=["psZ2"])
                S.op("act", lambda e, q4=q4: e.activation(out=gsum[0:1, q4 * 512:(q4 + 1) * 512], in_=psZ[2][0:1, 0:512], func=AF.Copy),
                     reads=["psZ2"], writes=["gsum"])
            gs4 = gsum[0:1, :].rearrange("o (n j c) -> o n j c", j=2, c=32)
            S.op("dve", lambda e: e.tensor_tensor(out=gate_s[0:1, :].rearrange("o (n c) -> o n c", c=32), in0=gs4[:, :, 0, :], in1=gs4[:, :, 1, :], op=ALU.add),
                 reads=["gsum"], writes=["gate_s"])
            gcn = gate_s[0:1, :].rearrange("o (n c) -> o c n", c=32)
            for c in range(32):
                S.op("dve", lambda e, c=c: e.max(out=top8s[0:1, c, :], in_=gcn[:, c, :]), reads=["gate_s"], writes=["top8s.%d" % c])
            S.op("dve", lambda e: e.tensor_tensor(out=biasr[0:1, :].rearrange("o (n c) -> o n c", c=32), in0=gate_s[0:1, :].rearrange("o (n c) -> o n c", c=32),
                                                  in1=top8s[0:1, :, 2:3].rearrange("o c k -> o k c").to_broadcast([1, 32, 32]), op=ALU.is_lt),
                 reads=["gate_s"] + ["top8s.%d" % c for c in range(32)], writes=["biasr"])
            S.op("dve", lambda e: e.tensor_scalar(out=biasr[0:1, :], in0=biasr[0:1, :], scalar1=NEG, scalar2=None, op0=ALU.mult), reads=["biasr"], writes=["biasr"])
            S4 = S_all[:].rearrange("p (n j) c -> p n j c", j=2)
            for q4 in range(4):
                S.op("pe", lambda e, q4=q4: e.matmul(psZ[2][:, 0:256], lhsT=ones128[0:1, :], rhs=biasr[0:1, q4 * 256:(q4 + 1) * 256], start=True, stop=True),
                     reads=["biasr", "ones128"], writes=["psZ2"])
                S.op("dve", lambda e, q4=q4: e.tensor_tensor(out=S4[:, q4 * 8:(q4 + 1) * 8, :, :], in0=S4[:, q4 * 8:(q4 + 1) * 8, :, :],
                                                             in1=psZ[2][:, 0:256].rearrange("p (n c) -> p n c", c=32).unsqueeze(2).to_broadcast([128, 8, 2, 32]), op=ALU.add),
                     reads=["psZ2", "S_all"], writes=["S_all"])
            S.op("act", lambda e: e.activation(out=PTA[:].rearrange("p g c -> p (g c)"), in_=Sf, func=AF.Exp), reads=["S_all"], writes=[kpta])
            for h in range(8):
                S.op("pe", lambda e, h=h: e.matmul(psZ[2][0:4, 328 + 4 * h:332 + 4 * h], lhsT=KTs[0:64, h, 4 * b:4 * b + 4], rhs=QTs[0:64, h, 4 * b:4 * b + 4],
                                                   start=True, stop=True), reads=["KTs", "QTs"], writes=["psZ2"])
            S.op("dve", lambda e: e.tensor_tensor(out=Sown[:].rearrange("k (h q) -> k h q", h=8), in0=psZ[2][0:4, 328:360].rearrange("k (h q) -> k h q", h=8),
                                                  in1=trib[0:4, 0:4].unsqueeze(1).to_broadcast([4, 8, 4]), op=ALU.add),
                 reads=["psZ2", "trib"], writes=["Sown"])
            S.op("act", lambda e: e.activation(out=Pown[:], in_=Sown[:], func=AF.Exp), reads=["Sown"], writes=["Pown"])

        def item_V(b, n):
            i = vcnt[0]; vcnt[0] += 1
            slot = i % 3
            VB = vblk[slot]; kv = "vblk%d" % slot
            PTA = PT_all[b % 2]; kpta = "PT_all%d" % (b % 2)
            for r in range(2):
                S.op("pe", lambda e, r=r: e.matmul(psZ[3][0:32, :], lhsT=PTA[:, 2 * n + r, :], rhs=VB[:, r, :], start=(n == 0 and r == 0), stop=False),
                     reads=[kv, kpta], writes=["psZ3"])

        def item_FIN(b):
            PTA = PT_all[b % 2]; kpta = "PT_all%d" % (b % 2)
            S.op("pe", lambda e: e.matmul(psZ[3][0:32, :], lhsT=Pown[:, :], rhs=vown[0:4, b, :], start=False, stop=True),
                 reads=["Pown", "vown"], writes=["psZ3"])
            for g in range(64):
                S.op("pe", lambda e, g=g: e.matmul(psZ[2][0:32, 320:321], lhsT=PTA[:, g, :], rhs=onec[:, 1:2], start=(g == 0), stop=False),
                     reads=[kpta, "onesb"], writes=["psZ2"])
            S.op("pe", lambda e: e.matmul(psZ[2][0:32, 320:321], lhsT=Pown[:, :], rhs=onec[0:4, 1:2], start=False, stop=True),
                 reads=["Pown", "onesb"], writes=["psZ2"])
            S.op("dve", lambda e: e.reciprocal(out=dens[:, 0:1], in_=psZ[2][0:32, 320:321]), reads=["psZ2"], writes=["dens"])
            S.op("dve", lambda e: e.tensor_tensor(out=oc[:], in0=psZ[3][0:32, :], in1=bmask[:], op=ALU.mult), reads=["psZ3", "bmask"], writes=["oc"])
            S.op("dve", lambda e: e.tensor_reduce(out=o64[:], in_=oc[:].rearrange("c (h d) -> c d h", h=8), axis=AX.X, op=ALU.add), reads=["oc"], writes=["o64"])
            S.op("dve", lambda e: e.tensor_scalar(out=o64b[:], in0=o64[:], scalar1=dens[:, 0:1], scalar2=None, op0=ALU.mult), reads=["o64", "dens"], writes=["o64b"])
            S.op("pe", lambda e: e.transpose(out=psT[0][0:64, 0:32], in_=o64b[:], identity=idn[0:32, 0:32]), reads=["o64b", "idn"], writes=["psT0"])
            S.op("act", lambda e: e.activation(out=oTs[0:64, :, 4 * b:4 * b + 4], in_=psT[0][0:64, 0:32].rearrange("d (h q) -> d h q", h=8), func=AF.Copy),
                 reads=["psT0"], writes=["oTs"])

        items = []
        for b in range(4):
            for n in range(32):
                items.append(("K", b, n))
                if b > 0:
                    items.append(("V", b - 1, n))
            if b > 0:
                items.append(("FIN", b - 1, 0))
            items.append(("GATE", b, 0))
        for n in range(32):
            items.append(("V", 3, n))
        items.append(("FIN", 3, 0))
        gather_list = [it for it in items if it[0] in ("K", "V")]
        gpos = [0]
        gslot = {"K": 0, "V": 0}

        def issue_gathers(upto):
            while gpos[0] < min(upto, len(gather_list)):
                kind, b, n = gather_list[gpos[0]]
                gather(kind.lower(), b, n, gslot[kind] % 3)
                gslot[kind] += 1
                gpos[0] += 1

        ipos = [0]
        gdone = [0]

        def run_items(k):
            for _ in range(k):
                if ipos[0] >= len(items):
                    return
                kind, b, n = items[ipos[0]]; ipos[0] += 1
                if kind == "K":
                    item_K(b, n)
                if kind == "V":
                    item_V(b, n)
                if kind in ("K", "V"):
                    gdone[0] += 1
                    issue_gathers(gdone[0] + PFD)
                if kind == "GATE":
                    flushK()
                    item_GATE(b)
                if kind == "FIN":
                    item_FIN(b)
        S.dma("pool", lambda e: e.dma_start(out=woB[:], in_=w_out[512:1024, :].rearrange("(h d) f -> d h f", d=64)), "woB", writes=["woB"])

        ctr = [0]

        pend = []

        def pump(flush=False):
            keep = []
            for ent in pend:
                ent[0] -= 1
                if ent[0] <= 0 or flush:
                    ent[1]()
                else:
                    keep.append(ent)
            pend[:] = keep

        def attn(ob, h):
            q0 = ob * 256
            kq = "QT.%d.%d" % (h, ob)
            PO = psA[(ob * 8 + h) % 2]
            kpo = "psA%d" % ((ob * 8 + h) % 2)
            fi = (ob * 8 + h) % 2
            OS, RC = o_sb[fi], rc[fi]
            units = [("own", ob)] + [("past", n) for n in range(ob)]

            def mk(ui, kind, n):
                i = ctr[0]; ctr[0] += 1
                PS = psZ[i % 2]; kps = "psZ%d" % (i % 2)
                PT = PTb[i % 3]; kpt = "PT%d" % (i % 3)
                last = (ui == len(units) - 1)
                if kind == "own":
                    c0 = 2 * ob * 128

                    def A():
                        S.op("pe", lambda e: e.matmul(PS[:, 0:128], lhsT=KT[0:72, h, c0:c0 + 128], rhs=QT[0:72, h, q0:q0 + 128],
                                                      start=True, stop=False), reads=[kq], writes=[kps])
                        S.op("pe", lambda e: e.matmul(PS[:, 0:128], lhsT=idn[:], rhs=trib[:], start=False, stop=True),
                             reads=["idn", "trib"], writes=[kps])
                        S.op("pe", lambda e: e.matmul(PS[:, 128:256], lhsT=KT[0:72, h, c0:c0 + 128], rhs=QT[0:72, h, q0 + 128:q0 + 256],
                                                      start=True, stop=True), reads=[kq], writes=[kps])
                        S.op("pe", lambda e: e.matmul(PS[:, 256:384], lhsT=KT[0:72, h, c0 + 128:c0 + 256],
                                                      rhs=QT[0:72, h, q0 + 128:q0 + 256], start=True, stop=False), reads=[kq], writes=[kps])
                        S.op("pe", lambda e: e.matmul(PS[:, 256:384], lhsT=idn[:], rhs=trib[:], start=False, stop=True),
                             reads=["idn", "trib"], writes=[kps])
                        S.op("act", lambda e: e.activation(out=PT[:, 0:384], in_=PS[:, 0:384], func=AF.Exp), reads=[kps], writes=[kpt])

                    def B():
                        S.op("pe", lambda e: e.matmul(PO[0:65, 0:256], lhsT=Vaug[:, 2 * ob, h, :], rhs=PT[:, 0:256],
                                                      start=True, stop=False), reads=[kpt], writes=[kpo])
                        S.op("pe", lambda e: e.matmul(PO[0:65, 128:256], lhsT=Vaug[:, 2 * ob + 1, h, :], rhs=PT[:, 256:384],
                                                      start=False, stop=last), reads=[kpt], writes=[kpo])
                else:
                    c0 = 2 * n * 128

                    def A():
                        S.op("pe", lambda e: e.matmul(PS[:, 0:256], lhsT=KT[0:72, h, c0:c0 + 128], rhs=QT[0:72, h, q0:q0 + 256],
                                                      start=True, stop=True), reads=[kq], writes=[kps])
                        S.op("pe", lambda e: e.matmul(PS[:, 256:512], lhsT=KT[0:72, h, c0 + 128:c0 + 256], rhs=QT[0:72, h, q0:q0 + 256],
                                                      start=True, stop=True), reads=[kq], writes=[kps])
                        S.op("act", lambda e: e.activation(out=PT[:, :], in_=PS[:, :], func=AF.Exp), reads=[kps], writes=[kpt])

                    def B():
                        S.op("pe", lambda e: e.matmul(PO[0:65, 0:256], lhsT=Vaug[:, 2 * n, h, :], rhs=PT[:, 0:256],
                                                      start=False, stop=False), reads=[kpt], writes=[kpo])
                        S.op("pe", lambda e: e.matmul(PO[0:65, 0:256], lhsT=Vaug[:, 2 * n + 1, h, :], rhs=PT[:, 256:512],
                                                      start=False, stop=last), reads=[kpt], writes=[kpo])
                return A, B, last

            def F1():
                S.op("act", lambda e: e.activation(out=OS[0:65, :], in_=PO[0:65, 0:256], func=AF.Copy), reads=[kpo], writes=["osb%d" % fi])
                S.op("dve", lambda e: e.reciprocal(out=RC[64:65, :], in_=OS[64:65, :]), reads=["osb%d" % fi], writes=["rc%d" % fi])

            def F2():
                S.op("pe", lambda e: e.matmul(psZ[2][0:64, 0:256], lhsT=onesf[64:65, 0:64], rhs=RC[64:65, :], start=True, stop=True),
                     reads=["rc%d" % fi, "onesf"], writes=["psZ2"])
                S.op("dve", lambda e: e.tensor_tensor(out=QT[0:64, h, q0:q0 + 256], in0=OS[0:64, :], in1=psZ[2][0:64, 0:256], op=ALU.mult),
                     reads=["osb%d" % fi, "psZ2"], writes=[kq])

            for ui, (kind, n) in enumerate(units):
                A, B, last = mk(ui, kind, n)
                A()
                if SAMPLE and ITEMS_PER_UNIT:
                    run_items(ITEMS_PER_UNIT)
                pump()
                if last:
                    pend.append([1, lambda B=B: (B(), F1())])
                    pend.append([2, F2])
                else:
                    pend.append([1, B])

        issue_gathers(PFD)
        for ob_ in range(8):
            for h_ in range(8):
                if not (FLAGS & 1024):
                    attn(ob_, h_)
                if SAMPLE and not ITEMS_PER_UNIT:
                    run_items(5)
        pump(flush=True)
        if SAMPLE:
            run_items(len(items))
        else:
            S.op("pool", lambda e: e.memset(oTs[:], 0.0), writes=["oTs"])
        S.barrier()

        if PHASES <= 2:
            return finish()
        def p3_tile(j):
            R = 128 if j < NT else NS
            r0 = 128 * j
            s = j % 2
            XT = xt3[s]; kx = "xt3_%d" % s
            ST = stt[s]; ks = "st3_%d" % s
            cur_par[0] = s
            S.dma("sp", lambda e, XT=XT, R=R, r0=r0: e.dma_start(out=XT[0:R, :], in_=x[r0:r0 + R, :]), kx, writes=[kx])
            for half in range(2):
                zb = (2 * j + half) % 4
                Z = psZ[zb]; kz = "psZ%d" % zb
                for g in range(4):
                    S.op("pe", lambda e, g=g, Z=Z, R=R, r0=r0, half=half: e.matmul(Z[0:R, :], lhsT=A_T[:, g, r0:r0 + R],
                                                                                    rhs=woA[:, g, half * 512:(half + 1) * 512], start=(g == 0), stop=False),
                         reads=["woA"], writes=[kz])
                for h in range(8):
                    lhs = QT[0:64, h, r0:r0 + R] if j < NT else oTs[0:64, h, 0:R]
                    S.op("pe", lambda e, h=h, Z=Z, R=R, lhs=lhs, half=half: e.matmul(Z[0:R, :], lhsT=lhs, rhs=woB[0:64, h, half * 512:(half + 1) * 512],
                                                                                      start=False, stop=(h == 7)),
                         reads=["woB", "oTs"], writes=[kz])
                S.op("dve", lambda e, Z=Z, R=R, j=j, half=half, XT=XT: e.tensor_tensor(out=h1[0:R, j, half * 512:(half + 1) * 512], in0=Z[0:R, :],
                                                                                        in1=XT[0:R, half * 512:(half + 1) * 512], op=ALU.add),
                     reads=[kz, kx], writes=["h1.%d.%d" % (j, half)])
            kh = "h1.%d" % j
            S.op("act", lambda e, R=R, j=j, ST=ST: e.activation(out=junk3[0:R, :], in_=h1[0:R, j, :], func=AF.Square, accum_out=ST[0:R, 0:1]),
                 reads=["h1.%d.0" % j, "h1.%d.1" % j], writes=["junk3", ks + ".0"])
            S.op("act", lambda e, R=R, ST=ST: e.activation(out=ST[0:R, 1:2], in_=ST[0:R, 0:1], func=AF.Ln, scale=1.0 / D, bias=epsc[0:R, :]),
                 reads=[ks + ".0"], writes=[ks + ".1"])
            S.op("act", lambda e, R=R, ST=ST: e.activation(out=ST[0:R, 2:3], in_=ST[0:R, 1:2], func=AF.Exp, scale=-0.5), reads=[ks + ".1"], writes=[ks + ".2"])
            S.op("act", lambda e, R=R, j=j, ST=ST, s=s: e.activation(out=hn3[s][0:R, :], in_=h1[0:R, j, :], func=AF.Copy, scale=ST[0:R, 2:3]),
                 reads=["h1.%d.0" % j, "h1.%d.1" % j, ks + ".2"], writes=["hn3_%d" % s])
            norm_T(hn3[s], "hn3_%d" % s, R, 8, hn2T[:, :, r0:r0 + R], "hn2T.%d" % j)

        pipelined(p3_tile, NT + 1, HEADF[1])
        cur_par[0] = None
        S.barrier()

        if PHASES <= 3:
            return finish()
        def load_quarter(qd):
            sl = qd % 2
            S.dma("pool", lambda e: e.dma_start(out=wu[sl][:], in_=w_up[:, qd * 1024:(qd + 1) * 1024].rearrange("(k p) f -> p k f", p=128)),
                  "wu%d" % sl, writes=["wu%d" % sl])
            S.dma("pool", lambda e: e.dma_start(out=wd[sl][:], in_=w_down[qd * 1024:(qd + 1) * 1024, :].rearrange("(k p) f -> p k f", p=128)),
                  "wd%d" % sl, writes=["wd%d" % sl])

        def p4_block(qd, m):
            sl = qd % 2
            N = 512 if m < 4 else NS
            c0 = 512 * m
            ab = (qd * 5 + m) % 2
            for fl in range(8):
                ui = uc[0]; uc[0] += 1
                Z = psZ[ui % 2]; kz = "psZ%d" % (ui % 2)
                RT = rt[ui % 3]; krt = "rt%d" % (ui % 3)
                for kc in range(8):
                    S.op("pe", lambda e, kc=kc, fl=fl, Z=Z, N=N, c0=c0: e.matmul(Z[:, 0:N], lhsT=wu[sl][:, kc, fl * 128:(fl + 1) * 128],
                                                                                  rhs=hn2T[:, kc, c0:c0 + N], start=(kc == 0), stop=(kc == 7)),
                         reads=["wu%d" % sl], writes=[kz])
                S.op("act", lambda e, Z=Z, RT=RT, N=N: e.activation(out=RT[:, 0:N], in_=Z[:, 0:N], func=AF.Relu), reads=[kz], writes=[krt])
                S.op("pool", lambda e, RT=RT, N=N, fl=fl, ab=ab: e.tensor_tensor(out=aT[ab][:, fl, 0:N], in0=RT[:, 0:N], in1=RT[:, 0:N], op=ALU.mult),
                     reads=[krt], writes=["aT%d.%d" % (ab, fl)])
            tiles = [4 * m + t for t in range(4)] if m < 4 else [NT]
            for ti, j in enumerate(tiles):
                R = 128 if j < NT else NS
                for half in range(2):
                    di_ = (qd * 5 + m) * 8 + ti * 2 + half
                    if di_ % 4 < 2:
                        Z = psZ[2 + di_ % 2]; kz = "psZ%d" % (2 + di_ % 2)
                    else:
                        Z = psA[di_ % 2]; kz = "psA%d" % (di_ % 2)
                    for fl in range(8):
                        S.op("pe", lambda e, fl=fl, Z=Z, R=R, ti=ti, half=half, ab=ab: e.matmul(
                            Z[0:R, :], lhsT=aT[ab][:, fl, ti * 128:ti * 128 + R], rhs=wd[sl][:, fl, half * 512:(half + 1) * 512],
                            start=(fl == 0), stop=(fl == 7)),
                            reads=["aT%d.%d" % (ab, fl), "wd%d" % sl], writes=[kz])
                    S.op("dve", lambda e, Z=Z, R=R, j=j, half=half: e.tensor_tensor(out=h1[0:R, j, half * 512:(half + 1) * 512],
                                                                                    in0=Z[0:R, :], in1=h1[0:R, j, half * 512:(half + 1) * 512], op=ALU.add),
                         reads=[kz], writes=["h1.%d.%d" % (j, half)])


        load_quarter(0)
        load_quarter(1)
        uc = [0]
        for qd in range(4):
            sl = qd % 2
            if qd == 3:
                S.dma("pool", lambda e: e.dma_start(out=wpg[:], in_=w_pg.rearrange("(k p) f -> p k f", p=128)), "wpg", writes=["wu0", "wd0", "wpg"])
                S.dma("pool", lambda e: e.dma_start(out=wpp[:], in_=w_pp.rearrange("(k p) f -> p k f", p=128)), "wpp", writes=["wu0", "wd0", "wpp"])
            for m_ in range(5):
                p4_block(qd, m_)
            if qd < 2:
                load_quarter(qd + 2)
        S.barrier()

        if PHASES <= 4:
            return finish()
        def p5_tile(j):
            R = 128 if j < NT else NS
            r0 = 128 * j
            s = j % 2
            ST = stt[s]; ks = "st5_%d" % s
            cur_par[0] = s
            S.dma("sp", lambda e, R=R, r0=r0, s=s: e.dma_start(out=pt5[s][0:R, :], in_=pin[r0:r0 + R, :]), "pt5_%d" % s, writes=["pt5_%d" % s])
            S.op("act", lambda e, R=R, j=j, ST=ST: e.activation(out=junk5[0:R, :], in_=h1[0:R, j, :], func=AF.Square, accum_out=ST[0:R, 0:1]),
                 reads=["h1.%d.0" % j, "h1.%d.1" % j], writes=["junk5", ks + ".0"])
            S.op("act", lambda e, R=R, ST=ST: e.activation(out=ST[0:R, 1:2], in_=ST[0:R, 0:1], func=AF.Ln, scale=1.0 / D, bias=epsc[0:R, :]),
                 reads=[ks + ".0"], writes=[ks + ".1"])
            S.op("act", lambda e, R=R, ST=ST: e.activation(out=ST[0:R, 2:3], in_=ST[0:R, 1:2], func=AF.Exp, scale=-0.5), reads=[ks + ".1"], writes=[ks + ".2"])
            S.op("act", lambda e, R=R, j=j, ST=ST, s=s: e.activation(out=hn5[s][0:R, :], in_=h1[0:R, j, :], func=AF.Copy, scale=ST[0:R, 2:3]),
                 reads=["h1.%d.0" % j, "h1.%d.1" % j, ks + ".2"], writes=["hn5_%d" % s])
            norm_T(hn5[s], "hn5_%d" % s, R, 16, hnT5[s][:, :, 0:R], "hnT5_%d" % s)
            S.op("act", lambda e, R=R, s=s: e.activation(out=pb5[s][0:R, :], in_=pt5[s][0:R, :], func=AF.Copy), reads=["pt5_%d" % s], writes=["pb5_%d" % s])
            tb = tbank()
            for kc in range(2):
                S.op("pe", lambda e, kc=kc, tb=tb, R=R, s=s: e.transpose(out=psT[tb][:, kc * 128:kc * 128 + R], in_=pb5[s][0:R, kc * 128:(kc + 1) * 128],
                                                                         identity=idn[0:R, 0:R]),
                     reads=["pb5_%d" % s, "idn"], writes=["psT%d" % tb])
            S.op("dve", lambda e, tb=tb, R=R, s=s: e.tensor_copy(out=pT5[s][:, :, 0:R], in_=psT[tb][:, 0:256].rearrange("p (k t) -> p k t", k=2)[:, :, 0:R]),
                 reads=["psT%d" % tb], writes=["pT5_%d" % s])
            for half in range(2):
                hs = slice(half * 512, (half + 1) * 512)
                ZG = psZ[2 * s + half]; kzg = "psZ%d" % (2 * s + half)
                ZE = psA[s]; kze = "psA%d" % s
                for kc in range(8):
                    S.op("pe", lambda e, kc=kc, ZG=ZG, R=R, s=s, hs=hs: e.matmul(ZG[0:R, :], lhsT=hnT5[s][:, kc, 0:R], rhs=wpg[:, kc, hs],
                                                                                  start=(kc == 0), stop=(kc == 7)),
                         reads=["hnT5_%d" % s, "wpg"], writes=[kzg])
                S.op("act", lambda e, ZG=ZG, R=R, s=s, hs=hs: e.activation(out=sig5[s][0:R, hs], in_=ZG[0:R, :], func=AF.Exp, scale=-1.0),
                     reads=[kzg], writes=["sig5_%d.%d" % (s, half)])
                S.op("dve", lambda e, R=R, s=s, hs=hs: e.tensor_scalar(out=sig5[s][0:R, hs], in0=sig5[s][0:R, hs], scalar1=1.0, scalar2=None, op0=ALU.add),
                     reads=["sig5_%d.%d" % (s, half)], writes=["sig5_%d.%d" % (s, half)])
                S.op("dve", lambda e, R=R, s=s, hs=hs: e.reciprocal(out=sig5[s][0:R, hs], in_=sig5[s][0:R, hs]),
                     reads=["sig5_%d.%d" % (s, half)], writes=["sig5_%d.%d" % (s, half)])
                for kc in range(2):
                    S.op("pe", lambda e, kc=kc, ZE=ZE, R=R, s=s, hs=hs: e.matmul(ZE[0:R, :], lhsT=pT5[s][:, kc, 0:R], rhs=wpp[:, kc, hs],
                                                                                  start=(kc == 0), stop=(kc == 1)),
                         reads=["pT5_%d" % s, "wpp"], writes=[kze])
                S.op("act", lambda e, ZE=ZE, R=R, s=s, hs=hs: e.activation(out=e5[s][0:R, hs], in_=ZE[0:R, :], func=AF.Copy),
                     reads=[kze], writes=["e5_%d.%d" % (s, half)])
            ke = ["e5_%d.0" % s, "e5_%d.1" % s]
            S.op("act", lambda e, R=R, s=s, ST=ST: e.activation(out=junk5[0:R, :], in_=e5[s][0:R, :], func=AF.Square, accum_out=ST[0:R, 4:5]),
                 reads=ke, writes=["junk5", ks + ".4"])
            S.op("act", lambda e, R=R, ST=ST: e.activation(out=ST[0:R, 5:6], in_=ST[0:R, 4:5], func=AF.Ln, scale=1.0 / D, bias=epsc[0:R, :]),
                 reads=[ks + ".4"], writes=[ks + ".5"])
            S.op("act", lambda e, R=R, ST=ST: e.activation(out=ST[0:R, 6:7], in_=ST[0:R, 5:6], func=AF.Exp, scale=-0.5), reads=[ks + ".5"], writes=[ks + ".6"])
            S.op("dve", lambda e, R=R, s=s: e.tensor_tensor(out=t5[s][0:R, :], in0=e5[s][0:R, :], in1=GP[0:R, :], op=ALU.mult),
                 reads=ke + ["gp"], writes=["t5_%d" % s])
            S.op("dve", lambda e, R=R, s=s, ST=ST: e.scalar_tensor_tensor(out=t5[s][0:R, :], in0=t5[s][0:R, :], scalar=ST[0:R, 6:7], in1=sig5[s][0:R, :],
                                                                          op0=ALU.mult, op1=ALU.mult),
                 reads=["t5_%d" % s, ks + ".6", "sig5_%d.0" % s, "sig5_%d.1" % s], writes=["t5_%d" % s])
            S.op("dve", lambda e, R=R, s=s, j=j: e.tensor_tensor(out=y5[s][0:R, :], in0=t5[s][0:R, :], in1=h1[0:R, j, :], op=ALU.add),
                 reads=["t5_%d" % s, "h1.%d.0" % j, "h1.%d.1" % j], writes=["y5_%d" % s])
            S.dma("sp", lambda e, R=R, r0=r0, s=s: e.dma_start(out=y[r0:r0 + R, :], in_=y5[s][0:R, :]), "o_y%d" % s, reads=["y5_%d" % s], final=True)

        pipelined(p5_tile, NT + 1, HEADF[2])
        cur_par[0] = None

        return finish()


_NC_CACHE = {}


def _consts():
    c = np.zeros((128, 896), np.float32)
    c[:, 0:128] = np.eye(128, dtype=np.float32)
    c[:, 128:256] = np.tril(np.ones((128, 128), np.float32))
    kq = np.arange(128)
    c[:, 256:384] = np.where(kq[:, None] > kq[None, :], NEG, 0.0)
    oh = np.zeros((8, 8, 8), np.float32)
    for n in range(8):
        oh[n, :, n] = 1.0
    c[:, 384:896] = oh.reshape(1, 512)
    return c


def _bmask():
    m = np.zeros((32, 512), np.float32)
    for c in range(32):
        h = c // 4
        m[c, h * 64:(h + 1) * 64] = 1.0
    return m


def extra_inputs(inp, c):
    pt = np.asarray(inp["page_table"], dtype=np.int32)[4 * c:4 * c + 4]
    ck = np.ascontiguousarray(np.asarray(inp["cache_k"], dtype=np.float32)).reshape(NPOOL * 64, 1024)
    cv = np.ascontiguousarray(np.asarray(inp["cache_v"], dtype=np.float32)).reshape(NPOOL * 64, 1024)
    return dict(cache_k=ck, cache_v=cv,
                pt_even=np.ascontiguousarray(pt[:, 0::2]).reshape(128), pt_odd=np.ascontiguousarray(pt[:, 1::2]).reshape(128),
                iota64=(np.arange(128) % 64).astype(np.float32).reshape(128, 1), bmask=_bmask())


def kernel(x_prompt, x_sample, p_prompt, p_sample, cache_k, cache_v, page_table,
           ln1, w_in, a_v_norm, a_ws, a_bs, q_norm, k_norm, w_out,
           ln2, w_up, w_down, ln3, w_ple_gate, w_ple_proj, ple_norm):
    f = lambda a: np.ascontiguousarray(np.asarray(a, dtype=np.float32))
    if "nc" not in _NC_CACHE:
        _NC_CACHE["nc"] = build_nc()
    nc = _NC_CACHE["nc"]
    xs = f(x_sample).reshape(NCORES, NS, D)
    ps = f(p_sample)[0].reshape(NCORES, NS, 256)
    xp = f(x_prompt)
    pp = f(p_prompt)[0]
    shared = dict(
        consts=_consts(), ln1=f(ln1)[0], ln2=f(ln2)[0], ln3=f(ln3)[0], w_in=f(w_in)[0],
        a_v_norm=f(a_v_norm)[0].reshape(512), a_ws=f(a_ws)[0], a_bs=f(a_bs)[0],
        q_norm=f(q_norm)[0], k_norm=f(k_norm)[0], w_out=f(w_out)[0], w_up=f(w_up)[0],
        w_down=f(w_down)[0], w_pg=f(w_ple_gate)[0], w_pp=f(w_ple_proj)[0], ple_norm=f(ple_norm)[0],
    )
    in_maps = []
    for c in range(NCORES):
        m = dict(shared)
        m["x"] = np.concatenate([xp[c], xs[c]], axis=0)
        m["p"] = np.concatenate([pp[c], ps[c]], axis=0)
        m.update(extra_inputs(dict(page_table=page_table, cache_k=cache_k, cache_v=cache_v), c))
        in_maps.append(m)
    res = run_bass_kernel_spmd(nc, in_maps, core_ids=list(range(NCORES)))
    R = res.results
    y = np.stack([r["y"] for r in R])
    kk = np.stack([r["k_new"] for r in R])
    vv = np.stack([r["v_new"] for r in R])
    ast = np.stack([r["a_state"] for r in R])
    y_prompt = np.ascontiguousarray(y[:, :SEQ])
    y_sample = np.ascontiguousarray(y[:, SEQ:].reshape(32, 4, D))
    k_prompt = kk[:, :SEQ].reshape(1, 8, SEQ, 8, 64)
    v_prompt = vv[:, :SEQ].reshape(1, 8, SEQ, 8, 64)
    k_sample = kk[:, SEQ:].reshape(1, 32, 4, 8, 64)
    v_sample = vv[:, SEQ:].reshape(1, 32, 4, 8, 64)
    a_s = ast.reshape(1, 32, 4, 4, 128)
    return (y_prompt, y_sample, np.ascontiguousarray(k_prompt), np.ascontiguousarray(v_prompt),
            np.ascontiguousarray(k_sample), np.ascontiguousarray(v_sample), np.ascontiguousarray(a_s))
```
